# Optimizing a Trainium2 kernel written in Bass

```python
import jax, jax.numpy as jnp
from jax import lax
import numpy as np

D_MODEL = 2048
BATCH = 4
SEQ = 2048
DEPTH = 2

GRID_W = 64
WIN_H_MAX = 8
WIN_W = 16
A_HEAD_DIM = 128
D_A = D_MODEL // 2
A_HEADS = D_A // A_HEAD_DIM
CHUNK = 128
D_B = D_MODEL // 4
B_GROUPS = 4
B_GROUP_DIM = D_B // B_GROUPS
D_C = D_MODEL // 4
CONV_W = 31
D_MIX = D_A + D_B + D_C
SPLIT_SIZES = [D_A] * 4 + [D_B] * 3 + [D_C] * 3
D_IN = sum(SPLIT_SIZES)
EPS = 1e-6

kernel_name = 'hybrid_natten_gmlp_conformer_encoder'


def rms_norm(x, g):
    xf = x.astype(jnp.float32)
    y = xf * lax.rsqrt(jnp.mean(xf * xf, axis=-1, keepdims=True) + EPS)
    return (y * g.astype(jnp.float32)).astype(x.dtype)


def layer_norm(x, g, b):
    xf = x.astype(jnp.float32)
    mu = jnp.mean(xf, axis=-1, keepdims=True)
    xc = xf - mu
    var = jnp.mean(xc * xc, axis=-1, keepdims=True)
    y = xc * lax.rsqrt(var + EPS) * g.astype(jnp.float32) + b.astype(jnp.float32)
    return y.astype(x.dtype)


def neighbourhood_attention(q, k, v, rpb):
    bsz, t, _ = q.shape
    rows = t // GRID_W
    kh = min(WIN_H_MAX, rows)

    def to_grid(z):
        return z.reshape(bsz, rows, GRID_W, A_HEADS, A_HEAD_DIM).transpose(0, 3, 1, 2, 4)

    qg = to_grid(q) * (A_HEAD_DIM ** -0.5)
    kg = to_grid(k)
    vg = to_grid(v)
    cols = np.arange(GRID_W)
    col_start = np.clip(cols - WIN_W // 2, 0, GRID_W - WIN_W)
    col_idx = col_start[:, None] + np.arange(WIN_W)[None, :]
    dc = col_idx - cols[:, None] + (WIN_W - 1)
    rpb_f = rpb.astype(jnp.float32)

    def one_row(r):
        rs = jnp.clip(r - kh // 2, 0, rows - kh)
        k_rows = lax.dynamic_slice_in_dim(kg, rs, kh, axis=2)
        v_rows = lax.dynamic_slice_in_dim(vg, rs, kh, axis=2)
        k_win = k_rows[:, :, :, col_idx, :]
        v_win = v_rows[:, :, :, col_idx, :]
        q_row = lax.dynamic_index_in_dim(qg, r, axis=2, keepdims=False)
        s = jnp.einsum('bhwd,bhiwjd->bhwij', q_row, k_win).astype(jnp.float32)
        dr = rs + jnp.arange(kh) - r + (WIN_H_MAX - 1)
        bias = rpb_f[:, dr][:, :, dc].transpose(0, 2, 1, 3)
        s = s + bias[None]
        p = jax.nn.softmax(s.reshape(bsz, A_HEADS, GRID_W, kh * WIN_W), axis=-1)
        p = p.reshape(bsz, A_HEADS, GRID_W, kh, WIN_W).astype(v.dtype)
        return jnp.einsum('bhwij,bhiwjd->bhwd', p, v_win)

    out = lax.map(one_row, jnp.arange(rows))
    return out.transpose(1, 0, 3, 2, 4).reshape(bsz, t, D_A)


def spatial_gating(u, v, ln_g, ln_b, w_s, b_s):
    bsz, t, _ = u.shape
    u = jax.nn.gelu(u, approximate=False)
    v = layer_norm(jax.nn.gelu(v, approximate=False), ln_g, ln_b)
    vc = v.reshape(bsz, t // CHUNK, CHUNK, B_GROUPS, B_GROUP_DIM)
    s = jnp.einsum('gts,bnsgc->bntgc', w_s, vc) + b_s.T[None, None, :, :, None]
    return u * s.reshape(bsz, t, D_B)


def conformer_conv(a, b, conv_w, conv_b, ln_g, ln_b, pw_w, pw_b):
    h = a * jax.nn.sigmoid(b)
    h = lax.conv_general_dilated(
        h, conv_w, window_strides=(1,), padding=[(CONV_W // 2, CONV_W // 2)],
        dimension_numbers=('NWC', 'WIO', 'NWC'), feature_group_count=D_C) + conv_b
    h = jax.nn.silu(layer_norm(h, ln_g, ln_b))
    return h @ pw_w + pw_b


def setup_inputs(seed: int = 0) -> dict:
    key = jax.random.key(seed)
    ks = jax.random.split(key, 16)
    f32 = jnp.float32
    nrm = lambda k, shape, s: jax.random.normal(k, shape, f32) * s
    return {
        'x': nrm(ks[0], (BATCH, SEQ, D_MODEL), 1.0),
        'pre_norm_g': 1.0 + nrm(ks[1], (DEPTH, D_MODEL), 0.02),
        'w_in': nrm(ks[2], (DEPTH, D_MODEL, D_IN), D_MODEL ** -0.5),
        'attn_rpb': nrm(ks[3], (DEPTH, A_HEADS, 2 * WIN_H_MAX - 1, 2 * WIN_W - 1), 0.1),
        'sgu_ln_g': 1.0 + nrm(ks[4], (DEPTH, D_B), 0.02),
        'sgu_ln_b': nrm(ks[5], (DEPTH, D_B), 0.01),
        'sgu_w': nrm(ks[6], (DEPTH, B_GROUPS, CHUNK, CHUNK), CHUNK ** -0.5),
        'sgu_b': 1.0 + nrm(ks[7], (DEPTH, B_GROUPS, CHUNK), 0.01),
        'conv_w': nrm(ks[8], (DEPTH, CONV_W, 1, D_C), CONV_W ** -0.5),
        'conv_b': nrm(ks[9], (DEPTH, D_C), 0.01),
        'conv_ln_g': 1.0 + nrm(ks[10], (DEPTH, D_C), 0.02),
        'conv_ln_b': nrm(ks[11], (DEPTH, D_C), 0.01),
        'conv_pw_w': nrm(ks[12], (DEPTH, D_C, D_C), D_C ** -0.5),
        'conv_pw_b': nrm(ks[13], (DEPTH, D_C), 0.01),
        'w_out': nrm(ks[14], (DEPTH, D_MIX, D_MODEL), D_MIX ** -0.5),
        'post_norm_g': 1.0 + nrm(ks[15], (DEPTH, D_MODEL), 0.02),
    }


def reference(x, pre_norm_g, w_in, attn_rpb, sgu_ln_g, sgu_ln_b, sgu_w, sgu_b,
              conv_w, conv_b, conv_ln_g, conv_ln_b, conv_pw_w, conv_pw_b,
              w_out, post_norm_g):
    offsets = np.cumsum(SPLIT_SIZES)[:-1].tolist()
    for l in range(DEPTH):
        h = rms_norm(x, pre_norm_g[l])
        z = h @ w_in[l]
        q, k, v, g_a, u_b, v_b, g_b, a_c, b_c, g_c = jnp.split(z, offsets, axis=-1)
        y_a = neighbourhood_attention(q, k, v, attn_rpb[l]) * jax.nn.silu(g_a)
        y_b = spatial_gating(u_b, v_b, sgu_ln_g[l], sgu_ln_b[l], sgu_w[l], sgu_b[l]) * jax.nn.silu(g_b)
        y_c = conformer_conv(a_c, b_c, conv_w[l], conv_b[l], conv_ln_g[l], conv_ln_b[l],
                             conv_pw_w[l], conv_pw_b[l]) * jax.nn.silu(g_c)
        y = jnp.concatenate([y_a, y_b, y_c], axis=-1) @ w_out[l]
        x = x + rms_norm(y, post_norm_g[l])
    return x
```

```python
import contextlib
import numpy as np
import concourse.bass as bass
import concourse.mybir as mybir
from concourse.bass_utils import run_bass_kernel_spmd

F32 = mybir.dt.float32
BF16 = mybir.dt.bfloat16
AF = mybir.ActivationFunctionType
ALU = mybir.AluOpType

D = 2048
NH = 8
EPS = 1e-6
NEG = -30000.0
NSLOT_DRAM = 73
NPP = 140
LAYER_TILES = [(10, 12), (8, 10)]
TKMAX = 12 * 128
TFMAX = 10 * 128


class Op:
    __slots__ = ("eng", "fn", "deps", "dma_key", "pos", "need_inc", "milestone", "dma_cnt")

    def __init__(self, eng, fn, deps, dma_key):
        self.eng = eng
        self.fn = fn
        self.deps = deps
        self.dma_key = dma_key
        self.pos = -1
        self.need_inc = False
        self.milestone = 0
        self.dma_cnt = 0


class Prog:
    ENGS = ("pe", "act", "dve", "pool", "sp")

    def __init__(self, nc):
        self.nc = nc
        self.ops = []
        self.res_w = {}
        self.res_r = {}

    def add(self, eng, fn, reads=(), writes=(), dma_key=None):
        reads = list(reads)
        writes = list(writes)
        pbk = [k for k in reads if isinstance(k, tuple) and k[0] == "pb"]
        if pbk:
            reads = [k for k in reads if not (isinstance(k, tuple) and k[0] == "pb")]
            writes = writes + [k for k in pbk if k not in writes]
        if any(isinstance(k, tuple) and k[0] == "ar" for k in reads + writes):
            reads.append("ARENA")
        idx = len(self.ops)
        deps = set()
        for r in reads:
            w = self.res_w.get(r)
            if w is not None:
                deps.add(w)
        for w in writes:
            lw = self.res_w.get(w)
            if lw is not None:
                deps.add(lw)
            for rd in self.res_r.get(w, ()):
                deps.add(rd)
        for r in reads:
            self.res_r.setdefault(r, []).append(idx)
        for w in writes:
            self.res_w[w] = idx
            self.res_r[w] = []
        deps.discard(idx)
        self.ops.append(Op(eng, fn, deps, dma_key))
        return idx

    def emit(self):
        nc = self.nc
        ops = self.ops
        per_eng = {e: [] for e in self.ENGS}
        dma_counts = {}
        for i, op in enumerate(ops):
            op.pos = len(per_eng[op.eng])
            per_eng[op.eng].append(i)
            if op.dma_key is not None:
                dma_counts[op.dma_key] = dma_counts.get(op.dma_key, 0) + 1
                op.dma_cnt = dma_counts[op.dma_key]
        waited = {}
        waits = [None] * len(ops)
        for i, op in enumerate(ops):
            wl = []
            best = {}
            for d in op.deps:
                p = ops[d]
                if p.dma_key is not None:
                    key = ("dma", p.dma_key)
                    val = p.dma_cnt
                else:
                    if p.eng == "pe" and op.eng == "pe" and op.dma_key is None:
                        continue
                    key = ("eng", p.eng)
                    val = p.pos
                if val > best.get(key, (-1, None))[0]:
                    best[key] = (val, d)
            for key, (val, d) in best.items():
                wk = (op.eng, key)
                if waited.get(wk, -1) >= val:
                    continue
                waited[wk] = val
                wl.append(d)
                if ops[d].dma_key is None:
                    ops[d].need_inc = True
            waits[i] = wl
        for e in self.ENGS:
            m = 0
            for i in per_eng[e]:
                op = ops[i]
                if op.dma_key is None and op.need_inc:
                    m += 1
                    op.milestone = m
        self.stats = {e: len(per_eng[e]) for e in self.ENGS}
        with contextlib.ExitStack() as es:
            esem = {e: es.enter_context(nc.semaphore("s_" + e)) for e in self.ENGS}
            dsem = {k: es.enter_context(nc.semaphore("d_%d" % n)) for n, k in enumerate(dma_counts)}
            block = es.enter_context(nc.Block())

            def run(e, eng):
                for i in per_eng[e]:
                    op = ops[i]
                    for d in waits[i]:
                        p = ops[d]
                        if p.dma_key is not None:
                            eng.wait_ge(dsem[p.dma_key], 16 * p.dma_cnt)
                        else:
                            eng.wait_ge(esem[p.eng], p.milestone)
                    if op.fn is None:
                        continue
                    ins = op.fn(eng)
                    if op.dma_key is not None:
                        ins.then_inc(dsem[op.dma_key], 16)
                    elif op.need_inc:
                        ins.then_inc(esem[e], 1)

            @block.tensor
            def _(eng):
                run("pe", eng)

            @block.scalar
            def _(eng):
                run("act", eng)

            @block.vector
            def _(eng):
                run("dve", eng)

            @block.gpsimd
            def _(eng):
                run("pool", eng)

            @block.sync
            def _(eng):
                run("sp", eng)


def chunks(T, step=512):
    out = []
    t = 0
    while t < T:
        n = min(step, T - t)
        out.append((t, n))
        t += n
    return out


def tile_keys(t0, n):
    return [("hT", i) for i in range(t0 // 128, (t0 + n - 1) // 128 + 1)]


ALL_HT = [("hT", i) for i in range(12)]


def build_program(layer_ids, ext_in_tiles, ext_out_tiles):
    nc = bass.Bass("TRN2", target_bir_lowering=False)
    P = Prog(nc)
    dram = {}
    dram["x_in"] = nc.dram_tensor("x_in", [ext_in_tiles * 128, D], F32, kind="ExternalInput").ap()
    dram["x_out"] = nc.dram_tensor("x_out", [ext_out_tiles * 128, D], F32, kind="ExternalOutput").ap()
    for l in layer_ids:
        dram["wst", l] = nc.dram_tensor("wst%d" % l, [NSLOT_DRAM, 128, 2048], F32, kind="ExternalInput").ap()
        dram["bias", l] = nc.dram_tensor("bias%d" % l, [NH, 128, 1920], F32, kind="ExternalInput").ap()
        dram["gpre", l] = nc.dram_tensor("gpre%d" % l, [1, D], F32, kind="ExternalInput").ap()
        dram["gpost", l] = nc.dram_tensor("gpost%d" % l, [1, D], F32, kind="ExternalInput").ap()
        dram["ln512", l] = nc.dram_tensor("ln512_%d" % l, [1, 1024], F32, kind="ExternalInput").ap()
        dram["bsb", l] = nc.dram_tensor("bsb%d" % l, [128, 512], F32, kind="ExternalInput").ap()
        dram["wsT", l] = nc.dram_tensor("wsT%d" % l, [128, 512], F32, kind="ExternalInput").ap()
        dram["pp", l] = nc.dram_tensor("pp%d" % l, [128, NPP], F32, kind="ExternalInput").ap()
    if len(layer_ids) == 2:
        dram["x_mid"] = nc.dram_tensor("x_mid", [LAYER_TILES[1][1] * 128, D], F32).ap()

    def sb(name, shape, dt):
        return nc.alloc_sbuf_tensor(name, shape, dt).ap()

    hT = sb("hT", [128, 16 * TKMAX], BF16)
    YT = sb("YT", [128, 16 * TFMAX], BF16)
    ring = sb("ring", [128, 8 * 2048], BF16)
    ident = sb("ident", [128, 128], BF16)
    identf = sb("identf", [128, 128], F32)
    onesf = sb("onesf", [128, 128], F32)
    epsc = sb("epsc", [128, 1], F32)
    pp = sb("pp", [128, NPP], F32)
    ln512 = sb("ln512", [128, 1024], F32)
    bsb = sb("bsb", [128, 512], F32)
    wsT = sb("wsT", [128, 512], BF16)
    stat = sb("stat", [128, 64], F32)
    stat2 = sb("stat2", [128, 64], F32)
    bnst = sb("bnst", [128, 8], F32)
    mv = sb("mv", [128, 24], F32)
    ARENA_F32 = 18688
    arena = sb("arena", [128, ARENA_F32], F32)
    PS = nc.alloc_psum_tensor("ps", [128, 4096], F32).ap()

    class Carver:
        def __init__(self):
            self.off = 0

        def f32(self, n):
            a = arena[:, self.off:self.off + n]
            self.off += n
            assert self.off <= ARENA_F32, self.off
            return a

        def bf16(self, n):
            assert n % 2 == 0
            a = arena[:, self.off:self.off + n // 2].bitcast(BF16)
            self.off += n // 2
            assert self.off <= ARENA_F32, self.off
            return a

    def psq(c0, n):
        return [("pb", q) for q in range(c0 // 512, (c0 + n - 1) // 512 + 1)]

    def fence():
        P.add("dve", lambda e: e.memset(stat2[:, 63:64], 0.0), writes=["ARENA", "fence_cell"])

    P.add("pool", lambda e: e.memset(identf, 0.0), writes=["identf"])
    P.add("pool", lambda e: e.affine_select(out=identf, in_=identf, pattern=[[-1, 128]], compare_op=ALU.not_equal,
                                            fill=1.0, base=0, channel_multiplier=1), reads=["identf"], writes=["identf"])
    P.add("dve", lambda e: e.tensor_copy(out=ident, in_=identf), reads=["identf"], writes=["ident"])
    P.add("dve", lambda e: e.memset(onesf, 1.0 / 512.0), writes=["onesf"])
    P.add("dve", lambda e: e.memset(epsc, EPS), writes=["epsc"])

    items = []
    item_index = {}

    def add_item(l, tag, slot, n):
        item_index.setdefault((l, tag), []).append(len(items))
        items.append((l, slot, n))

    for l in layer_ids:
        ntf = LAYER_TILES[l][0]
        for h in range(NH):
            for j, nm in enumerate(("v", "k", "q", "g")):
                add_item(l, (nm, h), 4 * h + j, 1)
        add_item(l, "vb", 32, 4)
        add_item(l, "u", 36, 4)
        add_item(l, "gB", 40, 4)
        for ct in range(4):
            add_item(l, ("a", ct), 44 + 2 * ct, 1)
            add_item(l, ("b", ct), 45 + 2 * ct, 1)
        for ct in range(4):
            add_item(l, ("gC", ct), 52 + ct, 1)
        for cb in range(4):
            add_item(l, ("wo", 0, cb), 57 + 4 * cb, 4)
        for cb in (1, 0):
            add_item(l, ("wo", 1, cb), 57 + 4 * cb, 4)

    ring_slot = []
    pos = 0
    for (l, slot, n) in items:
        if n == 4:
            pos = (pos + 3) // 4 * 4
        ring_slot.append(pos % 8)
        pos += n
    occ = [None] * 8
    wstate = {"next": 0}

    def pump():
        while wstate["next"] < len(items):
            i = wstate["next"]
            l, slot, n = items[i]
            rs = ring_slot[i]
            if any(occ[rs + q] is not None for q in range(n)):
                return
            for q in range(n):
                occ[rs + q] = i
            dst = ring[:, rs * 2048:(rs + n) * 2048]
            if n == 1:
                src = dram["wst", l][slot]
            else:
                dst = dst.rearrange("p (s f) -> p s f", s=n)
                src = dram["wst", l][slot:slot + n].rearrange("s p f -> p s f")
            P.add("pool", (lambda d, s: lambda e: e.dma_start(out=d, in_=s))(dst, src),
                  writes=[("ring", rs + q) for q in range(n)], dma_key=("ring", rs))
            wstate["next"] += 1

    def w_use(l, tag, k=0):
        i = item_index[(l, tag)][k]
        assert i < wstate["next"], ("weight item not issued", l, tag)
        rs = ring_slot[i]
        n = items[i][2]
        return ring[:, rs * 2048:(rs + n) * 2048], [("ring", rs + q) for q in range(n)], i

    def w_release(i):
        n = items[i][2]
        rs = ring_slot[i]
        for q in range(n):
            assert occ[rs + q] == i
            occ[rs + q] = None
        pump()

    pump()

    def MM(out, lhsT, rhs, start, stop):
        return lambda e: e.matmul(out, lhsT=lhsT, rhs=rhs, start=start, stop=stop)

    def TR(out, in_):
        return lambda e: e.transpose(out=out, in_=in_, identity=ident)

    def ACT(out, in_, func, bias=None, scale=None, accum_out=None):
        kw = {}
        if bias is not None:
            kw["bias"] = bias
        if scale is not None:
            kw["scale"] = scale
        if accum_out is not None:
            kw["accum_out"] = accum_out
        return lambda e: e.activation(out=out, in_=in_, func=func, **kw)

    def TT(out, in0, in1, op):
        return lambda e: e.tensor_tensor(out=out, in0=in0, in1=in1, op=op)

    def TS(out, in0, s1, s2, op0, op1=None):
        if op1 is None:
            return lambda e: e.tensor_scalar(out=out, in0=in0, scalar1=s1, scalar2=None, op0=op0)
        return lambda e: e.tensor_scalar(out=out, in0=in0, scalar1=s1, scalar2=s2, op0=op0, op1=op1)

    def STT(out, in0, scalar, in1, op0, op1):
        return lambda e: e.scalar_tensor_tensor(out=out, in0=in0, scalar=scalar, in1=in1, op0=op0, op1=op1)

    def CP(out, in_):
        return lambda e: e.tensor_copy(out=out, in_=in_)

    def RCP(out, in_):
        return lambda e: e.reciprocal(out=out, in_=in_)

    main_bank = {"i": 0}

    def next_bank(nb):
        b = main_bank["i"] % nb
        main_bank["i"] += 1
        return b

    cv0 = Carver()
    xinD = [cv0.f32(2048) for _ in range(3)]
    xin0 = [cv0.f32(2048) for _ in range(2)]
    junk0 = cv0.bf16(2048)
    gpost_b = cv0.f32(2048)
    gpre_b = cv0.f32(2048)
    xn0 = [cv0.bf16(2048), cv0.bf16(2048)]
    stat0 = sb("stat0", [128, 64], F32)
    xbufs = [(xin0[0], ("ar", "xin0", 0)), (xin0[1], ("ar", "xin0", 1)), (xinD[0], ("ar", "xinD", 0)),
             (xinD[1], ("ar", "xinD", 1)), (xinD[2], ("ar", "xinD", 2))]

    def make_p0(l, x_src):
        NTK = LAYER_TILES[l][1]
        TK = NTK * 128
        hT3 = hT[:, 0:16 * TK].rearrange("p (k t) -> p k t", k=16)

        def prologue():
            P.add("sp", (lambda d, s: lambda e: e.dma_start(out=d, in_=s))(gpre_b, dram["gpre", l].partition_broadcast(128)),
                  writes=[("ar", "gpre")], dma_key="gpre")
            P.add("dve", lambda e: e.memset(stat0, 0.0), writes=["stat0"] + [("stat0", i) for i in range(64)], reads=["stat0"])

        def front_load(i):
            xbuf, xkey = xbufs[i % 5]
            P.add("sp", (lambda d, s: lambda e: e.dma_start(out=d, in_=s))(xbuf, x_src[i * 128:(i + 1) * 128, :]),
                  reads=[("xsrc", l, i)], writes=[xkey], dma_key=xkey)

        def front(i, load=True):
            pb = i % 2
            xbuf, xkey = xbufs[i % 5]
            if load:
                front_load(i)
            P.add("act", ACT(junk0, xbuf, AF.Square, accum_out=stat0[:, i:i + 1]),
                  reads=[xkey, "stat0"], writes=[("ar", "junk"), ("stat0", i)])
            P.add("act", ACT(stat0[:, 16 + i:17 + i], stat0[:, i:i + 1], AF.Sqrt, bias=epsc[:, 0:1], scale=1.0 / D),
                  reads=[("stat0", i), "epsc", "stat0"], writes=[("stat0", 16 + i)])
            P.add("dve", RCP(stat0[:, 32 + i:33 + i], stat0[:, 16 + i:17 + i]),
                  reads=[("stat0", 16 + i)], writes=[("stat0", 32 + i)])
            P.add("dve", STT(xn0[pb], xbuf, stat0[:, 32 + i:33 + i], gpre_b, ALU.mult, ALU.mult),
                  reads=[xkey, ("stat0", 32 + i), ("ar", "gpre")], writes=[("ar", "xn", pb)])

        def back(i):
            pb = i % 2
            for half in range(2):
                bank = 2 + 2 * pb + half
                pst = PS[:, bank * 512:(bank + 1) * 512].bitcast(BF16)
                for kk in range(8):
                    k = half * 8 + kk
                    P.add("pe", TR(pst[:, kk * 128:(kk + 1) * 128], xn0[pb][:, k * 128:(k + 1) * 128]),
                          reads=[("ar", "xn", pb), "ident"], writes=psq(bank * 512, 512))
                dst = hT3[:, half * 8:(half + 1) * 8, i * 128:(i + 1) * 128]
                srcv = pst.rearrange("p (k t) -> p k t", k=8)
                if half == 0:
                    P.add("act", (lambda d, s: lambda e: e.copy(out=d, in_=s))(dst, srcv),
                          reads=psq(bank * 512, 512), writes=[("hT", i)])
                else:
                    P.add("dve", CP(dst, srcv), reads=psq(bank * 512, 512) + [("hT", i)], writes=[("hT", i)])

        return prologue, front, back, front_load

    p0_hoisted = {}

    def emit_layer(l, x_src, x_dst, n_dst_tiles):
        NTF, NTK = LAYER_TILES[l]
        TF, TK = NTF * 128, NTK * 128
        hT3 = hT[:, 0:16 * TK].rearrange("p (k t) -> p k t", k=16)
        YT3 = YT[:, 0:16 * TF].rearrange("p (k t) -> p k t", k=16)

        def hTs(k, t0, n):
            return hT[:, k * TK + t0:k * TK + t0 + n]

        def YTs(k, t0, n):
            return YT[:, k * TF + t0:k * TF + t0 + n]

        P.add("sp", lambda e: e.dma_start(out=pp, in_=dram["pp", l]), writes=["pp"], dma_key="pp")
        P.add("sp", lambda e: e.dma_start(out=ln512, in_=dram["ln512", l].partition_broadcast(128)), writes=["ln512"], dma_key="ln512")
        P.add("sp", lambda e: e.dma_start(out=bsb, in_=dram["bsb", l]), writes=["bsb"], dma_key="bsb")
        P.add("pool", lambda e: e.dma_start(out=wsT, in_=dram["wsT", l]), writes=["wsT"], dma_key="wsT")
        cw = pp[:, 0:124].rearrange("p (c j) -> p c j", c=4)
        cb_ = pp[:, 124:128]
        clng = pp[:, 128:132]
        clnb = pp[:, 132:136]
        pwb = pp[:, 136:140]

        p0_pro, p0_front, p0_back, _fl = make_p0(l, x_src)
        nh = p0_hoisted.get(l, 0)
        if nh == 0:
            p0_pro()
            p0_front(0)
            nh = 1
        for i in range(NTK):
            if i + 1 >= nh and i + 1 < NTK:
                p0_front(i + 1)
            p0_back(i)

        fence()
        cv = Carver()
        KT = [cv.bf16(TKMAX), cv.bf16(TKMAX)]
        QT = [cv.bf16(TFMAX), cv.bf16(TFMAX)]
        GT = [cv.bf16(TFMAX), cv.bf16(TFMAX)]
        vT = [cv.bf16(TKMAX), cv.bf16(TKMAX)]
        Vh = [cv.bf16(12 * 130), cv.bf16(12 * 130)]
        biasb = [cv.f32(1920), cv.f32(1920)]
        sbS = [cv.f32(640), cv.f32(640)]
        PT = [cv.bf16(640), cv.bf16(640)]
        an = [cv.bf16(128), cv.bf16(128)]
        th = cv.f32(512)
        rc = stat2
        for pb in range(2):
            P.add("dve", (lambda a: lambda e: e.memset(a, 2.0))(Vh[pb]), writes=[("ar", "Vh", pb)])

        def proj_fm(l, tag, T, evac, nb=2):
            wap, wkeys, wi = w_use(l, tag)
            for (t0, n) in chunks(T):
                b = next_bank(nb)
                for k in range(16):
                    P.add("pe", MM(PS[:, b * 512:b * 512 + n], wap[:, k * 128:(k + 1) * 128], hTs(k, t0, n), k == 0, k == 15),
                          reads=wkeys + tile_keys(t0, n), writes=psq(b * 512, n))
                evac(b, t0, n)
                yield
            w_release(wi)

        def gen_proj(h):
            pb = h % 2

            def ev_v(b, t0, n):
                P.add("dve", CP(vT[pb][:, t0:t0 + n], PS[:, b * 512:b * 512 + n]), reads=psq(b * 512, n),
                      writes=[("ar", "vT", pb)])

            def ev_k(b, t0, n):
                P.add("act", (lambda d, s: lambda e: e.copy(out=d, in_=s))(KT[pb][:, t0:t0 + n], PS[:, b * 512:b * 512 + n]),
                      reads=psq(b * 512, n), writes=[("ar", "KT", pb)])

            def ev_q(b, t0, n):
                P.add("act", ACT(QT[pb][:, t0:t0 + n], PS[:, b * 512:b * 512 + n], AF.Copy, scale=float(128 ** -0.5)),
                      reads=psq(b * 512, n), writes=[("ar", "QT", pb)])

            def ev_g(b, t0, n):
                P.add("act", ACT(th[:, 0:n], PS[:, b * 512:b * 512 + n], AF.Tanh, scale=0.5),
                      reads=psq(b * 512, n), writes=[("ar", "th")])
                P.add("dve", STT(GT[pb][:, t0:t0 + n], th[:, 0:n], 1.0, PS[:, b * 512:b * 512 + n], ALU.add, ALU.mult),
                      reads=psq(b * 512, n) + [("ar", "th")], writes=[("ar", "GT", pb)])

            yield from proj_fm(l, ("v", h), TK, ev_v)
            Vv = Vh[pb].rearrange("p (i c) -> p i c", c=130)
            for g0 in range(0, NTK, 4):
                ng = min(4, NTK - g0)
                slot = (g0 // 4) % 2
                pst = PS[:, 7 * 512 + slot * 256:7 * 512 + slot * 256 + 256].bitcast(BF16)
                for ii in range(ng):
                    i = g0 + ii
                    P.add("pe", TR(pst[:, ii * 128:(ii + 1) * 128], vT[pb][:, i * 128:(i + 1) * 128]),
                          reads=[("ar", "vT", pb), "ident"], writes=psq(7 * 512 + slot * 256, 256))
                P.add("dve", CP(Vv[:, g0:g0 + ng, 0:128], pst[:, 0:ng * 128].rearrange("p (i c) -> p i c", c=128)),
                      reads=psq(7 * 512 + slot * 256, 256) + [("ar", "Vh", pb)], writes=[("ar", "Vh", pb)])
                yield
            yield from proj_fm(l, ("k", h), TK, ev_k)
            yield from proj_fm(l, ("q", h), TF, ev_q)
            yield from proj_fm(l, ("g", h), TF, ev_g)

        def gen_attn(h):
            pb = h % 2
            Vv = Vh[pb].rearrange("p (i c) -> p i c", c=130)
            P.add("sp", (lambda d, s: lambda e: e.dma_start(out=d, in_=s))(biasb[pb], dram["bias", l][h]),
                  writes=[("ar", "bias", pb)], dma_key=("bias", pb))
            SBASE = [2 * 512, 4 * 512]
            OBASE = [3 * 512 + 128, 5 * 512 + 128]

            def st1(m):
                s = m % 2
                ks = max(m - 2, 0)
                for b in range(4 if m < 2 else 5):
                    j = ks + b
                    P.add("pe", MM(PS[:, SBASE[s] + b * 128:SBASE[s] + (b + 1) * 128], KT[pb][:, j * 128:(j + 1) * 128],
                                   QT[pb][:, m * 128:(m + 1) * 128], True, True),
                          reads=[("ar", "KT", pb), ("ar", "QT", pb)], writes=psq(SBASE[s] + b * 128, 128))

            def st2(m):
                s = m % 2
                bs = min(m, 2)
                nc_ = 512 if m < 2 else 640
                P.add("dve", TT(sbS[s][:, 0:nc_], PS[:, SBASE[s]:SBASE[s] + nc_], biasb[pb][:, bs * 640:bs * 640 + nc_], ALU.add),
                      reads=psq(SBASE[s], nc_) + [("ar", "bias", pb)], writes=[("ar", "sbS", s)])
                P.add("act", ACT(PT[s][:, 0:nc_], sbS[s][:, 0:nc_], AF.Exp), reads=[("ar", "sbS", s)], writes=[("ar", "PT", s)])

            def st3(m):
                s = m % 2
                ks = max(m - 2, 0)
                nb_ = 4 if m < 2 else 5
                for b in range(nb_):
                    j = ks + b
                    P.add("pe", MM(PS[:, OBASE[s]:OBASE[s] + 129], PT[s][:, b * 128:(b + 1) * 128], Vv[:, j, 0:129], b == 0, b == nb_ - 1),
                          reads=[("ar", "PT", s), ("ar", "Vh", pb)], writes=psq(OBASE[s], 129))
                P.add("dve", RCP(rc[:, m:m + 1], PS[:, OBASE[s] + 128:OBASE[s] + 129]), reads=psq(OBASE[s], 129),
                      writes=[("rc", m)])
                P.add("act", ACT(an[s], PS[:, OBASE[s]:OBASE[s] + 128], AF.Copy, scale=rc[:, m:m + 1]),
                      reads=psq(OBASE[s], 129) + [("rc", m)], writes=[("ar", "an", s)])

            def st4(m):
                s = m % 2
                grp = (m // 4) % 2
                pst = PS[:, 6 * 512 + grp * 256:6 * 512 + grp * 256 + 256].bitcast(BF16)
                ii = m % 4
                P.add("pe", TR(pst[:, ii * 128:(ii + 1) * 128], an[s]), reads=[("ar", "an", s), "ident"],
                      writes=psq(6 * 512 + grp * 256, 256))
                if ii == 3 or m == NTF - 1:
                    m0 = m - ii
                    nn = (ii + 1) * 128
                    P.add("dve", TT(YTs(h, m0 * 128, nn), pst[:, 0:nn], GT[pb][:, m0 * 128:m0 * 128 + nn], ALU.mult),
                          reads=psq(6 * 512 + grp * 256, 256) + [("ar", "GT", pb)], writes=[("YT", h)])

            for step in range(NTF + 3):
                if step < NTF:
                    st1(step)
                if 0 <= step - 1 < NTF:
                    st2(step - 1)
                if 0 <= step - 2 < NTF:
                    st3(step - 2)
                if 0 <= step - 3 < NTF:
                    st4(step - 3)
                yield

        def interleave(g1, g2):
            a, b = True, True
            while a or b:
                if a:
                    try:
                        next(g1)
                    except StopIteration:
                        a = False
                if b:
                    try:
                        next(g2)
                    except StopIteration:
                        b = False

        def empty():
            return
            yield

        interleave(gen_proj(0), empty())
        for h in range(NH):
            interleave(gen_proj(h + 1) if h + 1 < NH else empty(), gen_attn(h))

        fence()
        cv = Carver()
        gall = cv.f32(NTF * 512)
        vln = cv.bf16(NTF * 512)
        sg = [cv.f32(512), cv.f32(512)]
        t1 = [cv.f32(512), cv.f32(512)]
        yb = [cv.bf16(512), cv.bf16(512)]
        gall3 = gall.rearrange("p (i c) -> p i c", c=512)
        vln3 = vln.rearrange("p (i c) -> p i c", c=512)
        lng_bc = ln512[:, 0:512]
        lnb_bc = ln512[:, 512:1024]

        def proj_tm(tag, evac):
            wap, wkeys, wi = w_use(l, tag)
            for i in range(NTF):
                b = next_bank(4)
                for k in range(16):
                    P.add("pe", MM(PS[:, b * 512:(b + 1) * 512], hTs(k, i * 128, 128), wap[:, k * 512:(k + 1) * 512], k == 0, k == 15),
                          reads=wkeys + [("hT", i)], writes=psq(b * 512, 512))
                evac(b, i)
            w_release(wi)

        def ev_vb(b, i):
            P.add("act", ACT(gall3[:, i, :], PS[:, b * 512:(b + 1) * 512], AF.Gelu), reads=psq(b * 512, 512),
                  writes=[("ar", "gall", i)])
            P.add("dve", (lambda o, s: lambda e: e.bn_stats(out=o, in_=s))(bnst[:, 0:6], gall3[:, i, :]),
                  reads=[("ar", "gall", i)], writes=["bnst"])
            P.add("dve", (lambda o, s: lambda e: e.bn_aggr(out=o, in_=s))(mv[:, 2 * i:2 * i + 2], bnst[:, 0:6]),
                  reads=["bnst"], writes=[("mv", i)])

        proj_tm("vb", ev_vb)
        mv3 = mv[:, 0:2 * NTF].rearrange("p (i c) -> p i c", c=2)
        P.add("act", ACT(stat[:, 0:NTF], mv3[:, :, 1], AF.Sqrt, bias=epsc[:, 0:1], scale=1.0),
              reads=[("mv", i) for i in range(NTF)] + ["epsc", "stat"], writes=["stat"])
        P.add("dve", RCP(stat[:, 16:16 + NTF], stat[:, 0:NTF]), reads=["stat"], writes=["stat"])
        for i in range(NTF):
            P.add("dve", TS(gall3[:, i, :], gall3[:, i, :], mv[:, 2 * i:2 * i + 1], stat[:, 16 + i:17 + i], ALU.subtract, ALU.mult),
                  reads=[("ar", "gall", i), ("mv", i), "stat"], writes=[("ar", "gall", i)])
            P.add("dve", TT(gall3[:, i, :], gall3[:, i, :], lng_bc, ALU.mult), reads=[("ar", "gall", i), "ln512"],
                  writes=[("ar", "gall", i)])
            P.add("dve", TT(vln3[:, i, :], gall3[:, i, :], lnb_bc, ALU.add), reads=[("ar", "gall", i), "ln512"],
                  writes=[("ar", "vln", i)])

        def ev_u(b, i):
            P.add("act", ACT(gall3[:, i, :], PS[:, b * 512:(b + 1) * 512], AF.Gelu), reads=psq(b * 512, 512),
                  writes=[("ar", "gall", i)])

        proj_tm("u", ev_u)

        def ev_gB(b, i):
            s = i % 2
            P.add("act", ACT(sg[s], PS[:, b * 512:(b + 1) * 512], AF.Silu), reads=psq(b * 512, 512),
                  writes=[("ar", "sg", s)])
            sbk = 4 + s
            for g in range(4):
                P.add("pe", MM(PS[:, sbk * 512 + g * 128:sbk * 512 + (g + 1) * 128], wsT[:, g * 128:(g + 1) * 128],
                               vln3[:, i, g * 128:(g + 1) * 128], True, True),
                      reads=["wsT", ("ar", "vln", i)], writes=psq(sbk * 512 + g * 128, 128))
            P.add("dve", TT(t1[s], PS[:, sbk * 512:(sbk + 1) * 512], bsb, ALU.add), reads=psq(sbk * 512, 512) + ["bsb"],
                  writes=[("ar", "t1", s)])
            P.add("dve", TT(t1[s], t1[s], gall3[:, i, :], ALU.mult), reads=[("ar", "t1", s), ("ar", "gall", i)],
                  writes=[("ar", "t1", s)])
            P.add("dve", TT(yb[s], t1[s], sg[s], ALU.mult), reads=[("ar", "t1", s), ("ar", "sg", s)],
                  writes=[("ar", "yb", s)])
            if i > 0:
                gB_back(i - 1)

        def gB_back(i):
            s = i % 2
            tb = (6 + s) * 512
            pst = PS[:, tb:tb + 256].bitcast(BF16)
            for g in range(4):
                P.add("pe", TR(pst[:, g * 128:(g + 1) * 128], yb[s][:, g * 128:(g + 1) * 128]),
                      reads=[("ar", "yb", s), "ident"], writes=psq(tb, 256))
            P.add("act", (lambda d, s_: lambda e: e.copy(out=d, in_=s_))(YT3[:, 8:12, i * 128:(i + 1) * 128],
                                                                      pst.rearrange("p (g t) -> p g t", g=4)),
                  reads=psq(tb, 256), writes=[("YT", 8 + g) for g in range(4)])

        proj_tm("gB", ev_gB)
        gB_back(NTF - 1)

        fence()
        cv = Carver()
        TC = TF + 16
        P.add("pool", (lambda d, s_: lambda e: e.dma_start(out=d, in_=s_))(ln512.bitcast(BF16), dram["wst", l][56]),
              writes=["ln512"], dma_key="pw")
        acc = cv.f32(4 * TFMAX)
        hcb = [cv.bf16(TFMAX + 32), cv.bf16(TFMAX + 32)]
        Dgs = [cv.bf16(31 * 128), cv.bf16(31 * 128)]
        sgm = cv.f32(512)
        sqb = [cv.f32(512), cv.f32(512)]
        mean_sb = cv.f32(TFMAX)
        tmpc = cv.f32(TFMAX)
        hn = cv.bf16(4 * TFMAX)
        accd = cv.f32(TFMAX)
        JP = 24
        pending = []

        def drain(k):
            for _ in range(min(k, len(pending))):
                pending.pop(0)()
        ident_b = ident.unsqueeze(1).to_broadcast([128, 31, 128])
        for pbh in range(2):
            P.add("dve", (lambda a: lambda e: e.memset(a, 0.0))(hcb[pbh][:, 0:16]), writes=[("ar", "hcb", pbh)])
        for ct in range(4):
            hb = hcb[ct % 2]
            hkey = ("ar", "hcb", ct % 2)
            Dg = Dgs[ct % 2]
            Dg3 = Dg.rearrange("p (j c) -> p j c", j=31)
            dkey = ("ar", "Dg", ct % 2)
            P.add("dve", TT(Dg3, ident_b, cw[:, ct, :].unsqueeze(2).to_broadcast([128, 31, 128]), ALU.mult),
                  reads=["ident", "pp"], writes=[dkey])
            wa, ka, ia = w_use(l, ("a", ct))
            wb, kb, ib = w_use(l, ("b", ct))
            for (t0, n) in chunks(TC):
                ba = (0, 1, 6, 7)[next_bank(4)]
                for k in range(16):
                    P.add("pe", MM(PS[:, ba * 512:ba * 512 + n], wa[:, k * 128:(k + 1) * 128], hTs(k, t0, n), k == 0, k == 15),
                          reads=ka + tile_keys(t0, n), writes=psq(ba * 512, n))
                bb = (0, 1, 6, 7)[next_bank(4)]
                for k in range(16):
                    P.add("pe", MM(PS[:, bb * 512:bb * 512 + n], wb[:, k * 128:(k + 1) * 128], hTs(k, t0, n), k == 0, k == 15),
                          reads=kb + tile_keys(t0, n), writes=psq(bb * 512, n))
                drain(3)
                P.add("act", ACT(sgm[:, 0:n], PS[:, bb * 512:bb * 512 + n], AF.Sigmoid), reads=psq(bb * 512, n),
                      writes=[("ar", "sgm")])
                P.add("dve", TT(hb[:, 15 + t0:15 + t0 + n], PS[:, ba * 512:ba * 512 + n], sgm[:, 0:n], ALU.mult),
                      reads=psq(ba * 512, n) + [("ar", "sgm"), hkey], writes=[hkey])
            w_release(ia)
            w_release(ib)
            drain(99)
            a_ct = acc[:, ct * TF:(ct + 1) * TF]
            for ci, (t0, n) in enumerate(chunks(TF)):
                bk = 2 + (ci % 2)
                for j in range(JP):
                    P.add("pe", MM(PS[:, bk * 512:bk * 512 + n], Dg[:, j * 128:(j + 1) * 128], hb[:, t0 + j:t0 + j + n], j == 0, j == JP - 1),
                          reads=[dkey, hkey], writes=psq(bk * 512, n))
                P.add("act", ACT(a_ct[:, t0:t0 + n], PS[:, bk * 512:bk * 512 + n], AF.Identity, bias=cb_[:, ct:ct + 1]),
                      reads=psq(bk * 512, n) + ["pp"], writes=[("ar", "acc", ct, ci)])
            def tap_op(j, ct=ct, hb=hb, hkey=hkey):
                if j == JP:
                    P.add("dve", TS(accd[:, 0:TF], hb[:, j:j + TF], cw[:, ct, j:j + 1], None, ALU.mult),
                          reads=[hkey, "pp"], writes=[("ar", "accd")])
                else:
                    P.add("dve", STT(accd[:, 0:TF], hb[:, j:j + TF], cw[:, ct, j:j + 1], accd[:, 0:TF], ALU.mult, ALU.add),
                          reads=[hkey, "pp", ("ar", "accd")], writes=[("ar", "accd")])

            def comb_op(ct=ct, a_ct=a_ct):
                nchk = len(chunks(TF))
                P.add("dve", TT(a_ct, a_ct, accd[:, 0:TF], ALU.add),
                      reads=[("ar", "acc", ct, ci) for ci in range(nchk)] + [("ar", "accd")],
                      writes=[("ar", "acc", ct, ci) for ci in range(nchk)])

            for j in range(JP, 31):
                pending.append((lambda j=j, f=tap_op: f(j)))
            pending.append(comb_op)
        drain(99)
        def gen_ln():
            chs = chunks(TF)
            banks = [(4, 5), (6, 7), (2, 3)]
            for ci, (t0, n) in enumerate(chs):
                ac = [acc[:, ct * TF + t0:ct * TF + t0 + n] for ct in range(4)]
                ms = mean_sb[:, t0:t0 + n]
                vs = tmpc[:, t0:t0 + n]
                P.add("dve", TT(ms, ac[0], ac[1], ALU.add), reads=[("ar", "acc", 0, ci), ("ar", "acc", 1, ci)],
                      writes=[("ar", "mean", ci)])
                P.add("act", ACT(vs, ac[0], AF.Square), reads=[("ar", "acc", 0, ci)], writes=[("ar", "tmpc", ci)])
                for ct in range(1, 4):
                    if ct >= 2:
                        P.add("dve", TT(ms, ms, ac[ct], ALU.add), reads=[("ar", "mean", ci), ("ar", "acc", ct, ci)],
                              writes=[("ar", "mean", ci)])
                    sb_ = sqb[ct % 2]
                    P.add("act", ACT(sb_[:, 0:n], ac[ct], AF.Square), reads=[("ar", "acc", ct, ci)], writes=[("ar", "sqb", ct % 2)])
                    P.add("dve", TT(vs, vs, sb_[:, 0:n], ALU.add), reads=[("ar", "tmpc", ci), ("ar", "sqb", ct % 2)],
                          writes=[("ar", "tmpc", ci)])
                yield
            for ci, (t0, n) in enumerate(chs):
                bm, bx = banks[ci]
                P.add("pe", MM(PS[:, bm * 512:bm * 512 + n], onesf, mean_sb[:, t0:t0 + n], True, True),
                      reads=["onesf", ("ar", "mean", ci)], writes=psq(bm * 512, n))
                P.add("pe", MM(PS[:, bx * 512:bx * 512 + n], onesf, tmpc[:, t0:t0 + n], True, True),
                      reads=["onesf", ("ar", "tmpc", ci)], writes=psq(bx * 512, n))
            yield
            for ci, (t0, n) in enumerate(chs):
                bm, bx = banks[ci]
                P.add("act", (lambda d, s: lambda e: e.copy(out=d, in_=s))(mean_sb[:, t0:t0 + n], PS[:, bm * 512:bm * 512 + n]),
                      reads=psq(bm * 512, n), writes=[("ar", "mean", ci)])
                P.add("dve", TT(sgm[:, 0:n], mean_sb[:, t0:t0 + n], mean_sb[:, t0:t0 + n], ALU.mult), reads=[("ar", "mean", ci)],
                      writes=[("ar", "sgm")])
                P.add("dve", TT(tmpc[:, t0:t0 + n], PS[:, bx * 512:bx * 512 + n], sgm[:, 0:n], ALU.subtract),
                      reads=psq(bx * 512, n) + [("ar", "sgm")], writes=[("ar", "tmpc", ci)])
                yield
            nch = len(chs)
            P.add("act", ACT(tmpc[:, 0:TF], tmpc[:, 0:TF], AF.Sqrt, bias=epsc[:, 0:1], scale=1.0),
                  reads=[("ar", "tmpc", ci) for ci in range(nch)] + ["epsc"], writes=[("ar", "tmpc", ci) for ci in range(nch)])
            yield
            for ci, (t0, n) in enumerate(chs):
                P.add("dve", RCP(tmpc[:, t0:t0 + n], tmpc[:, t0:t0 + n]), reads=[("ar", "tmpc", ci)], writes=[("ar", "tmpc", ci)])
                for ct in range(4):
                    a_c = acc[:, ct * TF + t0:ct * TF + t0 + n]
                    P.add("dve", TT(a_c, a_c, mean_sb[:, t0:t0 + n], ALU.subtract), reads=[("ar", "acc", ct, ci), ("ar", "mean", ci)],
                          writes=[("ar", "acc", ct, ci)])
                    P.add("dve", TT(a_c, a_c, tmpc[:, t0:t0 + n], ALU.mult), reads=[("ar", "acc", ct, ci), ("ar", "tmpc", ci)],
                          writes=[("ar", "acc", ct, ci)])
                    P.add("act", ACT(hn[:, ct * TF + t0:ct * TF + t0 + n], a_c, AF.Silu, bias=clnb[:, ct:ct + 1], scale=clng[:, ct:ct + 1]),
                          reads=[("ar", "acc", ct, ci), "pp"], writes=[("ar", "hn", ct, ci)])
                    if ct % 2 == 1:
                        yield


        def gen_gc():
            for ct in range(4):
                def ev_gc(b, t0, n, ct=ct):
                    P.add("act", ACT(YTs(12 + ct, t0, n), PS[:, b * 512:b * 512 + n], AF.Silu), reads=psq(b * 512, n),
                          writes=[("YT", 12 + ct)])

                yield from proj_fm(l, ("gC", ct), TF, ev_gc)

        interleave(gen_gc(), gen_ln())
        wpw = ln512.bitcast(BF16)
        kpw = ["ln512"]
        for ct in range(4):
            for (t0, n) in chunks(TF):
                b = next_bank(2)
                for k in range(4):
                    P.add("pe", MM(PS[:, b * 512:b * 512 + n], wpw[:, k * 512 + ct * 128:k * 512 + (ct + 1) * 128],
                                   hn[:, k * TF + t0:k * TF + t0 + n], k == 0, k == 3),
                          reads=kpw + [("ar", "hn", k, t0 // 512)], writes=psq(b * 512, n))
                P.add("dve", STT(YTs(12 + ct, t0, n), PS[:, b * 512:b * 512 + n], pwb[:, ct:ct + 1], YTs(12 + ct, t0, n), ALU.add, ALU.mult),
                      reads=psq(b * 512, n) + ["pp", ("YT", 12 + ct)], writes=[("YT", 12 + ct)])

        fence()
        NX = 3
        P.add("sp", (lambda d, s: lambda e: e.dma_start(out=d, in_=s))(gpost_b, dram["gpost", l].partition_broadcast(128)),
              writes=[("ar", "gpost")], dma_key="gpost")
        P.add("dve", lambda e: e.memset(stat, 0.0), writes=["stat"] + [("stat", i) for i in range(64)], reads=["stat"])
        G = (NTF + 1) // 2
        groups = [list(range(0, G)), list(range(G, NTF))]
        ysb = hT
        P.add("dve", lambda e: e.memset(stat2[:, 62:63], 0.0), writes=ALL_HT + ["ysb_claim"])
        xslot = {}
        kept_wo = {}

        xd = [(xinD[k], ("ar", "xinD", k)) for k in range(3)]
        if (l + 1) not in layer_ids:
            xd += [(xin0[k], ("ar", "xin0", k)) for k in range(2)]
        NX = len(xd)

        def emit_xload(t):
            sl = len(xslot) % NX
            xslot[t] = sl
            P.add("sp", (lambda d, s: lambda e: e.dma_start(out=d, in_=s))(xd[sl][0], x_src[t * 128:(t + 1) * 128, :]),
                  reads=[("xsrc", l, t)], writes=[xd[sl][1]], dma_key=xd[sl][1])

        for gi, grp in enumerate(groups):
            for t in grp[:NX]:
                emit_xload(t)
            nslab = (16 * TK * 2) // 8192
            ysl = ysb[:, 0:2 * 2048 * nslab].bitcast(F32)
            sl_of = [(ti + gi * len(groups[0])) % nslab for ti in range(len(grp))]
            for cidx, cbk in enumerate((0, 1, 2, 3) if gi == 0 else (3, 2, 1, 0)):
                if gi == 1 and cbk >= 2:
                    wo, ko, io = kept_wo[cbk]
                else:
                    wo, ko, io = w_use(l, ("wo", gi, cbk))
                for ti, t in enumerate(grp):
                    b = next_bank(4)
                    for k in range(16):
                        P.add("pe", MM(PS[:, b * 512:(b + 1) * 512], YTs(k, t * 128, 128), wo[:, k * 512:(k + 1) * 512], k == 0, k == 15),
                              reads=ko + [("YT", k)], writes=psq(b * 512, 512))
                    ys = sl_of[ti]
                    ydst = ysl[:, ys * 2048 + cbk * 512:ys * 2048 + (cbk + 1) * 512]
                    P.add("act", ACT(junk0[:, 0:512], PS[:, b * 512:(b + 1) * 512], AF.Square, accum_out=stat[:, 4 * t + cbk:4 * t + cbk + 1]),
                          reads=psq(b * 512, 512) + ["stat"], writes=[("ar", "junk"), ("stat", 4 * t + cbk)])
                    P.add("act", (lambda d, s_: lambda e: e.copy(out=d, in_=s_))(ydst, PS[:, b * 512:(b + 1) * 512]),
                          reads=psq(b * 512, 512) + ["ysb_claim"], writes=[("ysb", ys, cbk)])
                    if cidx == 3:
                        sl = xslot[t]
                        st4_ = stat[:, 4 * t:4 * t + 4]
                        P.add("dve", (lambda o, s_: lambda e: e.tensor_reduce(out=o, in_=s_, axis=mybir.AxisListType.X, op=ALU.add))(stat2[:, t:t + 1], st4_),
                              reads=[("stat", 4 * t + c) for c in range(4)], writes=[("s2", t)])
                        P.add("act", ACT(stat2[:, 16 + t:17 + t], stat2[:, t:t + 1], AF.Sqrt, bias=epsc[:, 0:1], scale=1.0 / D),
                              reads=[("s2", t), "epsc"], writes=[("s2", 16 + t)])
                        P.add("dve", RCP(stat2[:, 32 + t:33 + t], stat2[:, 16 + t:17 + t]), reads=[("s2", 16 + t)],
                              writes=[("s2", 32 + t)])
                        yt = ysl[:, ys * 2048:(ys + 1) * 2048]
                        P.add("dve", STT(yt, yt, stat2[:, 32 + t:33 + t], gpost_b, ALU.mult, ALU.mult),
                              reads=[("ysb", ys, c) for c in range(4)] + [("s2", 32 + t), ("ar", "gpost")],
                              writes=[("ysb", ys, c) for c in range(4)])
                        P.add("dve", TT(yt, yt, xd[sl][0], ALU.add),
                              reads=[("ysb", ys, c) for c in range(4)] + [xd[sl][1]],
                              writes=[("ysb", ys, c) for c in range(4)])
                        if t < n_dst_tiles:
                            P.add("sp", (lambda d, s_: lambda e: e.dma_start(out=d, in_=s_))(x_dst[t * 128:(t + 1) * 128, :], yt),
                                  reads=[("ysb", ys, c) for c in range(4)], writes=[("xsrc", l + 1, t)], dma_key=("xout", ys))
                        if ti + NX < len(grp):
                            emit_xload(grp[ti + NX])
                if gi == 0 and cbk >= 2:
                    kept_wo[cbk] = (wo, ko, io)
                else:
                    w_release(io)
                if gi == 1 and cidx == 1 and (l + 1) in layer_ids:
                    nfront(0, load=False)
                    nfront(1, load=False)
            if gi == 0 and (l + 1) in layer_ids:
                npro, nfront, _, nload = make_p0(l + 1, x_dst)
                npro()
                nload(0)
                nload(1)
                p0_hoisted[l + 1] = 2
        P.add("dve", lambda e: e.memset(stat2[:, 61:62], 0.0),
              writes=ALL_HT + [("ysb", ti, c) for ti in range(8) for c in range(4)])

    with nc.allow_low_precision("bf16 matmul operands, fp32 accumulation"):
        if len(layer_ids) == 2:
            emit_layer(0, dram["x_in"], dram["x_mid"], LAYER_TILES[1][1])
            emit_layer(1, dram["x_mid"], dram["x_out"], 8)
            final_keys = [("xsrc", 2, t) for t in range(8)]
        else:
            l = layer_ids[0]
            emit_layer(l, dram["x_in"], dram["x_out"], ext_out_tiles)
            final_keys = [("xsrc", l + 1, t) for t in range(ext_out_tiles)]
        P.add("sp", None, reads=final_keys)
        P.emit()
    return nc, P


def _small(w, c0):
    blk = w[:, c0:c0 + 128]
    return blk.reshape(16, 128, 128).transpose(1, 0, 2).reshape(128, 2048)


def _big(w, c0):
    blk = w[:, c0:c0 + 512]
    b = blk.reshape(16, 128, 512).transpose(1, 0, 2).reshape(128, 8192)
    return b.reshape(128, 4, 2048).transpose(1, 0, 2)


def prep_wst(w_in, pw, w_out):
    wst = np.empty((NSLOT_DRAM, 128, 2048), np.float32)
    for h in range(NH):
        wst[4 * h + 0] = _small(w_in, 2048 + h * 128)
        wst[4 * h + 1] = _small(w_in, 1024 + h * 128)
        wst[4 * h + 2] = _small(w_in, 0 + h * 128)
        wst[4 * h + 3] = _small(w_in, 3072 + h * 128)
    wst[32:36] = _big(w_in, 4608)
    wst[36:40] = _big(w_in, 4096)
    wst[40:44] = _big(w_in, 5120)
    for ct in range(4):
        wst[44 + 2 * ct] = _small(w_in, 5632 + ct * 128)
        wst[45 + 2 * ct] = _small(w_in, 6144 + ct * 128)
        wst[52 + ct] = _small(w_in, 6656 + ct * 128)
    wst[56] = pw.reshape(4, 128, 512).transpose(1, 0, 2).reshape(128, 2048)
    for cb in range(4):
        wst[57 + 4 * cb:61 + 4 * cb] = _big(w_out, cb * 512)
    return wst


def prep_bias(rpb, half):
    out = np.full((NH, 128, 3, 5, 128), NEG, np.float32)
    loc = np.arange(128)
    for s in range(3):
        tq = s * 128 + loc
        gq = tq if half == 0 else 2047 - tq
        qr, qc = gq // 64, gq % 64
        rs = np.clip(qr - 4, 0, 24)
        cs = np.clip(qc - 8, 0, 48)
        for b in range(5):
            tk = b * 128 + loc
            gk = tk if half == 0 else 2047 - tk
            kr, kc = gk // 64, gk % 64
            KR, QR = kr[:, None], qr[None, :]
            KC, QC = kc[:, None], qc[None, :]
            ok = (KR >= rs[None, :]) & (KR < rs[None, :] + 8) & (KC >= cs[None, :]) & (KC < cs[None, :] + 16)
            dr = np.clip(KR - QR + 7, 0, 14)
            dc = np.clip(KC - QC + 15, 0, 30)
            vals = rpb[:, dr, dc]
            blk = out[:, :, s, b, :]
            blk[:, ok] = vals[:, ok]
    return out.reshape(NH, 128, 1920)


def prep_core_inputs(inp, layer_ids, half):
    m = {}
    for l in layer_ids:
        w = inp["sgu_w"][l]
        wT = w.transpose(0, 2, 1)
        bs = inp["sgu_b"][l]
        cwl = inp["conv_w"][l][:, 0, :]
        if half == 1:
            wT = wT[:, ::-1, ::-1]
            bs = bs[:, ::-1]
            cwl = cwl[::-1]
        m["wsT%d" % l] = np.ascontiguousarray(wT.transpose(1, 0, 2).reshape(128, 512))
        m["bsb%d" % l] = np.ascontiguousarray(np.repeat(bs.T[:, :, None], 128, axis=2).reshape(128, 512))
        ppa = np.empty((128, NPP), np.float32)
        ppa[:, 0:124] = cwl.T.reshape(4, 128, 31).transpose(1, 0, 2).reshape(128, 124)
        ppa[:, 124:128] = inp["conv_b"][l].reshape(4, 128).T
        ppa[:, 128:132] = inp["conv_ln_g"][l].reshape(4, 128).T
        ppa[:, 132:136] = inp["conv_ln_b"][l].reshape(4, 128).T
        ppa[:, 136:140] = inp["conv_pw_b"][l].reshape(4, 128).T
        m["pp%d" % l] = ppa
        m["ln512_%d" % l] = np.concatenate([inp["sgu_ln_g"][l], inp["sgu_ln_b"][l]])[None, :].astype(np.float32)
        m["gpre%d" % l] = np.ascontiguousarray(inp["pre_norm_g"][l][None, :])
        m["gpost%d" % l] = np.ascontiguousarray(inp["post_norm_g"][l][None, :])
        m["bias%d" % l] = prep_bias(inp["attn_rpb"][l], half)
    return m


FUSED = True
_cache = {}


def _get_prog(key, *args):
    if key not in _cache:
        _cache[key] = build_program(*args)[0]
    return _cache[key]


def kernel(**inputs):
    inp = {k: np.asarray(v, dtype=np.float32) for k, v in inputs.items()}
    x = inp["x"]
    wsts = [prep_wst(inp["w_in"][l], inp["conv_pw_w"][l], inp["w_out"][l]) for l in range(2)]
    ntk0 = LAYER_TILES[0][1]
    core_maps = []
    for c in range(8):
        b, half = c // 2, c % 2
        xs = x[b] if half == 0 else x[b, ::-1]
        m = prep_core_inputs(inp, [0, 1], half)
        m["x_in"] = np.ascontiguousarray(xs[:ntk0 * 128])
        m["wst0"], m["wst1"] = wsts[0], wsts[1]
        core_maps.append(m)
    cores = list(range(8))
    if FUSED:
        nc = _get_prog("fused", [0, 1], ntk0, 8)
        res = run_bass_kernel_spmd(nc, core_maps, core_ids=cores)
        outs = [np.asarray(r["x_out"]) for r in res.results]
    else:
        ntk1 = LAYER_TILES[1][1]
        ncA = _get_prog("L0", [0], ntk0, ntk1)
        keysA = ["x_in", "wst0", "bias0", "gpre0", "gpost0", "ln512_0", "bsb0", "wsT0", "pp0"]
        resA = run_bass_kernel_spmd(ncA, [{k: m[k] for k in keysA} for m in core_maps], core_ids=cores)
        ncB = _get_prog("L1", [1], ntk1, 8)
        keysB = ["wst1", "bias1", "gpre1", "gpost1", "ln512_1", "bsb1", "wsT1", "pp1"]
        mapsB = []
        for c in range(8):
            mm = {k: core_maps[c][k] for k in keysB}
            mm["x_in"] = np.asarray(resA.results[c]["x_out"])
            mapsB.append(mm)
        resB = run_bass_kernel_spmd(ncB, mapsB, core_ids=cores)
        outs = [np.asarray(r["x_out"]) for r in resB.results]
    out = np.empty((4, 2048, 2048), np.float32)
    for c in range(8):
        b, half = c // 2, c % 2
        if half == 0:
            out[b, 0:1024] = outs[c]
        else:
            out[b, 1024:2048] = outs[c][::-1]
    return out
```

```python
import contextlib
import numpy as np
import concourse.bass as bass
import concourse.mybir as mybir
from concourse.bass_utils import run_bass_kernel_spmd

F32 = mybir.dt.float32
BF16 = mybir.dt.bfloat16
AF = mybir.ActivationFunctionType
ALU = mybir.AluOpType

D = 2048
NH = 8
EPS = 1e-6
NEG = -30000.0
NSLOT_DRAM = 73
NPP = 140
LAYER_TILES = [(10, 12), (8, 10)]
TKMAX = 12 * 128
TFMAX = 10 * 128


class Op:
    __slots__ = ("eng", "fn", "deps", "dma_key", "pos", "need_inc", "milestone", "dma_cnt")

    def __init__(self, eng, fn, deps, dma_key):
        self.eng = eng
        self.fn = fn
        self.deps = deps
        self.dma_key = dma_key
        self.pos = -1
        self.need_inc = False
        self.milestone = 0
        self.dma_cnt = 0


class Prog:
    ENGS = ("pe", "act", "dve", "pool", "sp")

    def __init__(self, nc):
        self.nc = nc
        self.ops = []
        self.res_w = {}
        self.res_r = {}

    def add(self, eng, fn, reads=(), writes=(), dma_key=None):
        reads = list(reads)
        writes = list(writes)
        pbk = [k for k in reads if isinstance(k, tuple) and k[0] == "pb"]
        if pbk:
            reads = [k for k in reads if not (isinstance(k, tuple) and k[0] == "pb")]
            writes = writes + [k for k in pbk if k not in writes]
        if any(isinstance(k, tuple) and k[0] == "ar" for k in reads + writes):
            reads.append("ARENA")
        idx = len(self.ops)
        deps = set()
        for r in reads:
            w = self.res_w.get(r)
            if w is not None:
                deps.add(w)
        for w in writes:
            lw = self.res_w.get(w)
            if lw is not None:
                deps.add(lw)
            for rd in self.res_r.get(w, ()):
                deps.add(rd)
        for r in reads:
            self.res_r.setdefault(r, []).append(idx)
        for w in writes:
            self.res_w[w] = idx
            self.res_r[w] = []
        deps.discard(idx)
        self.ops.append(Op(eng, fn, deps, dma_key))
        return idx

    def emit(self):
        nc = self.nc
        ops = self.ops
        per_eng = {e: [] for e in self.ENGS}
        dma_counts = {}
        for i, op in enumerate(ops):
            op.pos = len(per_eng[op.eng])
            per_eng[op.eng].append(i)
            if op.dma_key is not None:
                dma_counts[op.dma_key] = dma_counts.get(op.dma_key, 0) + 1
                op.dma_cnt = dma_counts[op.dma_key]
        waited = {}
        waits = [None] * len(ops)
        for i, op in enumerate(ops):
            wl = []
            best = {}
            for d in op.deps:
                p = ops[d]
                if p.dma_key is not None:
                    key = ("dma", p.dma_key)
                    val = p.dma_cnt
                else:
                    if p.eng == "pe" and op.eng == "pe" and op.dma_key is None:
                        continue
                    key = ("eng", p.eng)
                    val = p.pos
                if val > best.get(key, (-1, None))[0]:
                    best[key] = (val, d)
            for key, (val, d) in best.items():
                wk = (op.eng, key)
                if waited.get(wk, -1) >= val:
                    continue
                waited[wk] = val
                wl.append(d)
                if ops[d].dma_key is None:
                    ops[d].need_inc = True
            waits[i] = wl
        for e in self.ENGS:
            m = 0
            for i in per_eng[e]:
                op = ops[i]
                if op.dma_key is None and op.need_inc:
                    m += 1
                    op.milestone = m
        self.stats = {e: len(per_eng[e]) for e in self.ENGS}
        with contextlib.ExitStack() as es:
            esem = {e: es.enter_context(nc.semaphore("s_" + e)) for e in self.ENGS}
            dsem = {k: es.enter_context(nc.semaphore("d_%d" % n)) for n, k in enumerate(dma_counts)}
            block = es.enter_context(nc.Block())

            def run(e, eng):
                for i in per_eng[e]:
                    op = ops[i]
                    for d in waits[i]:
                        p = ops[d]
                        if p.dma_key is not None:
                            eng.wait_ge(dsem[p.dma_key], 16 * p.dma_cnt)
                        else:
                            eng.wait_ge(esem[p.eng], p.milestone)
                    if op.fn is None:
                        continue
                    ins = op.fn(eng)
                    if op.dma_key is not None:
                        ins.then_inc(dsem[op.dma_key], 16)
                    elif op.need_inc:
                        ins.then_inc(esem[e], 1)

            @block.tensor
            def _(eng):
                run("pe", eng)

            @block.scalar
            def _(eng):
                run("act", eng)

            @block.vector
            def _(eng):
                run("dve", eng)

            @block.gpsimd
            def _(eng):
                run("pool", eng)

            @block.sync
            def _(eng):
                run("sp", eng)


def chunks(T, step=512):
    out = []
    t = 0
    while t < T:
        n = min(step, T - t)
        out.append((t, n))
        t += n
    return out


def bchunks(T):
    k = (T + 511) // 512
    base = ((T + k - 1) // k + 15) // 16 * 16
    out = []
    t = 0
    while t < T:
        n = min(base, T - t)
        out.append((t, n))
        t += n
    return out


def tile_keys(t0, n):
    return [("hT", i) for i in range(t0 // 128, (t0 + n - 1) // 128 + 1)]


ALL_HT = [("hT", i) for i in range(12)]


def build_program(layer_ids, ext_in_tiles, ext_out_tiles):
    nc = bass.Bass("TRN2", target_bir_lowering=False)
    P = Prog(nc)
    dram = {}
    dram["x_in"] = nc.dram_tensor("x_in", [ext_in_tiles * 128, D], F32, kind="ExternalInput").ap()
    dram["x_out"] = nc.dram_tensor("x_out", [ext_out_tiles * 128, D], F32, kind="ExternalOutput").ap()
    for l in layer_ids:
        dram["wst", l] = nc.dram_tensor("wst%d" % l, [NSLOT_DRAM, 128, 2048], F32, kind="ExternalInput").ap()
        dram["bias", l] = nc.dram_tensor("bias%d" % l, [NH, 128, 1920], F32, kind="ExternalInput").ap()
        dram["gpre", l] = nc.dram_tensor("gpre%d" % l, [1, D], F32, kind="ExternalInput").ap()
        dram["gpost", l] = nc.dram_tensor("gpost%d" % l, [1, D], F32, kind="ExternalInput").ap()
        dram["ln512", l] = nc.dram_tensor("ln512_%d" % l, [1, 1024], F32, kind="ExternalInput").ap()
        dram["bsb", l] = nc.dram_tensor("bsb%d" % l, [128, 512], F32, kind="ExternalInput").ap()
        dram["wsT", l] = nc.dram_tensor("wsT%d" % l, [128, 512], F32, kind="ExternalInput").ap()
        dram["pp", l] = nc.dram_tensor("pp%d" % l, [128, NPP], F32, kind="ExternalInput").ap()
    if len(layer_ids) == 2:
        dram["x_mid"] = nc.dram_tensor("x_mid", [LAYER_TILES[1][1] * 128, D], F32).ap()

    def sb(name, shape, dt):
        return nc.alloc_sbuf_tensor(name, shape, dt).ap()

    hT = sb("hT", [128, 16 * TKMAX], BF16)
    YT = sb("YT", [128, 16 * TFMAX], BF16)
    ring = sb("ring", [128, 8 * 2048], BF16)
    ident = sb("ident", [128, 128], BF16)
    identf = sb("identf", [128, 128], F32)
    onesf = sb("onesf", [128, 128], F32)
    epsc = sb("epsc", [128, 1], F32)
    pp = sb("pp", [128, NPP], F32)
    ln512 = sb("ln512", [128, 1024], F32)
    bsb = sb("bsb", [128, 512], F32)
    wsT = sb("wsT", [128, 512], BF16)
    stat = sb("stat", [128, 64], F32)
    stat2 = sb("stat2", [128, 64], F32)
    bnst = sb("bnst", [128, 8], F32)
    mv = sb("mv", [128, 24], F32)
    ARENA_F32 = 18688
    arena = sb("arena", [128, ARENA_F32], F32)
    PS = nc.alloc_psum_tensor("ps", [128, 4096], F32).ap()

    class Carver:
        def __init__(self):
            self.off = 0

        def f32(self, n):
            a = arena[:, self.off:self.off + n]
            self.off += n
            assert self.off <= ARENA_F32, self.off
            return a

        def bf16(self, n):
            assert n % 2 == 0
            a = arena[:, self.off:self.off + n // 2].bitcast(BF16)
            self.off += n // 2
            assert self.off <= ARENA_F32, self.off
            return a

    def psq(c0, n):
        return [("pb", q) for q in range(c0 // 512, (c0 + n - 1) // 512 + 1)]

    def fence():
        P.add("dve", lambda e: e.memset(stat2[:, 63:64], 0.0), writes=["ARENA", "fence_cell"])

    P.add("pool", lambda e: e.memset(identf, 0.0), writes=["identf"])
    P.add("pool", lambda e: e.affine_select(out=identf, in_=identf, pattern=[[-1, 128]], compare_op=ALU.not_equal,
                                            fill=1.0, base=0, channel_multiplier=1), reads=["identf"], writes=["identf"])
    P.add("dve", lambda e: e.tensor_copy(out=ident, in_=identf), reads=["identf"], writes=["ident"])
    P.add("dve", lambda e: e.memset(onesf, 1.0 / 512.0), writes=["onesf"])
    P.add("dve", lambda e: e.memset(epsc, EPS), writes=["epsc"])

    items = []
    item_index = {}

    def add_item(l, tag, slot, n):
        item_index.setdefault((l, tag), []).append(len(items))
        items.append((l, slot, n))

    for l in layer_ids:
        ntf = LAYER_TILES[l][0]
        for h in range(NH):
            for j, nm in enumerate(("v", "k", "q", "g")):
                add_item(l, (nm, h), 4 * h + j, 1)
        add_item(l, "vb", 32, 4)
        add_item(l, "u", 36, 4)
        add_item(l, "gB", 40, 4)
        for ct in range(4):
            add_item(l, ("a", ct), 44 + 2 * ct, 1)
            add_item(l, ("b", ct), 45 + 2 * ct, 1)
        for ct in range(4):
            add_item(l, ("gC", ct), 52 + ct, 1)
        for cb in range(4):
            add_item(l, ("wo", 0, cb), 57 + 4 * cb, 4)
        for cb in (1, 0):
            add_item(l, ("wo", 1, cb), 57 + 4 * cb, 4)

    ring_slot = []
    pos = 0
    for (l, slot, n) in items:
        if n == 4:
            pos = (pos + 3) // 4 * 4
        ring_slot.append(pos % 8)
        pos += n
    occ = [None] * 8
    wstate = {"next": 0}

    def pump():
        while wstate["next"] < len(items):
            i = wstate["next"]
            l, slot, n = items[i]
            rs = ring_slot[i]
            if any(occ[rs + q] is not None for q in range(n)):
                return
            for q in range(n):
                occ[rs + q] = i
            dst = ring[:, rs * 2048:(rs + n) * 2048]
            if n == 1:
                src = dram["wst", l][slot]
            else:
                dst = dst.rearrange("p (s f) -> p s f", s=n)
                src = dram["wst", l][slot:slot + n].rearrange("s p f -> p s f")
            P.add("pool", (lambda d, s: lambda e: e.dma_start(out=d, in_=s))(dst, src),
                  writes=[("ring", rs + q) for q in range(n)], dma_key=("ring", rs))
            wstate["next"] += 1

    def w_use(l, tag, k=0):
        i = item_index[(l, tag)][k]
        assert i < wstate["next"], ("weight item not issued", l, tag)
        rs = ring_slot[i]
        n = items[i][2]
        return ring[:, rs * 2048:(rs + n) * 2048], [("ring", rs + q) for q in range(n)], i

    def w_release(i):
        n = items[i][2]
        rs = ring_slot[i]
        for q in range(n):
            assert occ[rs + q] == i
            occ[rs + q] = None
        pump()

    pump()

    def MM(out, lhsT, rhs, start, stop):
        return lambda e: e.matmul(out, lhsT=lhsT, rhs=rhs, start=start, stop=stop)

    def TR(out, in_):
        return lambda e: e.transpose(out=out, in_=in_, identity=ident)

    def ACT(out, in_, func, bias=None, scale=None, accum_out=None):
        kw = {}
        if bias is not None:
            kw["bias"] = bias
        if scale is not None:
            kw["scale"] = scale
        if accum_out is not None:
            kw["accum_out"] = accum_out
        return lambda e: e.activation(out=out, in_=in_, func=func, **kw)

    def TT(out, in0, in1, op):
        return lambda e: e.tensor_tensor(out=out, in0=in0, in1=in1, op=op)

    def TS(out, in0, s1, s2, op0, op1=None):
        if op1 is None:
            return lambda e: e.tensor_scalar(out=out, in0=in0, scalar1=s1, scalar2=None, op0=op0)
        return lambda e: e.tensor_scalar(out=out, in0=in0, scalar1=s1, scalar2=s2, op0=op0, op1=op1)

    def STT(out, in0, scalar, in1, op0, op1):
        return lambda e: e.scalar_tensor_tensor(out=out, in0=in0, scalar=scalar, in1=in1, op0=op0, op1=op1)

    def CP(out, in_):
        return lambda e: e.tensor_copy(out=out, in_=in_)

    def RCP(out, in_):
        return lambda e: e.reciprocal(out=out, in_=in_)

    main_bank = {"i": 0}

    def next_bank(nb):
        b = main_bank["i"] % nb
        main_bank["i"] += 1
        return b

    cv0 = Carver()
    xinD = [cv0.f32(2048) for _ in range(3)]
    xin0 = [cv0.f32(2048) for _ in range(2)]
    junk0 = cv0.bf16(2048)
    gpost_b = cv0.f32(2048)
    gpre_b = cv0.f32(2048)
    xn0 = [cv0.bf16(2048), cv0.bf16(2048)]
    stat0 = sb("stat0", [128, 64], F32)
    xbufs = [(xin0[0], ("ar", "xin0", 0)), (xin0[1], ("ar", "xin0", 1)), (xinD[0], ("ar", "xinD", 0)),
             (xinD[1], ("ar", "xinD", 1)), (xinD[2], ("ar", "xinD", 2))]

    def make_p0(l, x_src):
        NTK = LAYER_TILES[l][1]
        TK = NTK * 128
        hT3 = hT[:, 0:16 * TK].rearrange("p (k t) -> p k t", k=16)

        def prologue():
            P.add("sp", (lambda d, s: lambda e: e.dma_start(out=d, in_=s))(gpre_b, dram["gpre", l].partition_broadcast(128)),
                  writes=[("ar", "gpre")], dma_key="gpre")
            P.add("dve", lambda e: e.memset(stat0, 0.0), writes=["stat0"] + [("stat0", i) for i in range(64)], reads=["stat0"])

        def front_load(i):
            xbuf, xkey = xbufs[i % 5]
            P.add("sp", (lambda d, s: lambda e: e.dma_start(out=d, in_=s))(xbuf, x_src[i * 128:(i + 1) * 128, :]),
                  reads=[("xsrc", l, i)], writes=[xkey], dma_key=xkey)

        def front(i, load=True):
            pb = i % 2
            xbuf, xkey = xbufs[i % 5]
            if load:
                front_load(i)
            P.add("act", ACT(junk0, xbuf, AF.Square, accum_out=stat0[:, i:i + 1]),
                  reads=[xkey, "stat0"], writes=[("ar", "junk"), ("stat0", i)])
            P.add("act", ACT(stat0[:, 16 + i:17 + i], stat0[:, i:i + 1], AF.Sqrt, bias=epsc[:, 0:1], scale=1.0 / D),
                  reads=[("stat0", i), "epsc", "stat0"], writes=[("stat0", 16 + i)])
            P.add("dve", RCP(stat0[:, 32 + i:33 + i], stat0[:, 16 + i:17 + i]),
                  reads=[("stat0", 16 + i)], writes=[("stat0", 32 + i)])
            P.add("dve", STT(xn0[pb], xbuf, stat0[:, 32 + i:33 + i], gpre_b, ALU.mult, ALU.mult),
                  reads=[xkey, ("stat0", 32 + i), ("ar", "gpre")], writes=[("ar", "xn", pb)])

        def back(i):
            pb = i % 2
            for half in range(2):
                bank = 2 + 2 * pb + half
                pst = PS[:, bank * 512:(bank + 1) * 512].bitcast(BF16)
                for kk in range(8):
                    k = half * 8 + kk
                    P.add("pe", TR(pst[:, kk * 128:(kk + 1) * 128], xn0[pb][:, k * 128:(k + 1) * 128]),
                          reads=[("ar", "xn", pb), "ident"], writes=psq(bank * 512, 512))
                dst = hT3[:, half * 8:(half + 1) * 8, i * 128:(i + 1) * 128]
                srcv = pst.rearrange("p (k t) -> p k t", k=8)
                if half == 0:
                    P.add("act", (lambda d, s: lambda e: e.copy(out=d, in_=s))(dst, srcv),
                          reads=psq(bank * 512, 512), writes=[("hT", i)])
                else:
                    P.add("dve", CP(dst, srcv), reads=psq(bank * 512, 512) + [("hT", i)], writes=[("hT", i)])

        return prologue, front, back, front_load

    p0_hoisted = {}

    def emit_layer(l, x_src, x_dst, n_dst_tiles):
        NTF, NTK = LAYER_TILES[l]
        TF, TK = NTF * 128, NTK * 128
        hT3 = hT[:, 0:16 * TK].rearrange("p (k t) -> p k t", k=16)
        YT3 = YT[:, 0:16 * TF].rearrange("p (k t) -> p k t", k=16)

        def hTs(k, t0, n):
            return hT[:, k * TK + t0:k * TK + t0 + n]

        def YTs(k, t0, n):
            return YT[:, k * TF + t0:k * TF + t0 + n]

        P.add("sp", lambda e: e.dma_start(out=pp, in_=dram["pp", l]), writes=["pp"], dma_key="pp")
        P.add("sp", lambda e: e.dma_start(out=ln512, in_=dram["ln512", l].partition_broadcast(128)), writes=["ln512"], dma_key="ln512")
        P.add("sp", lambda e: e.dma_start(out=bsb, in_=dram["bsb", l]), writes=["bsb"], dma_key="bsb")
        P.add("pool", lambda e: e.dma_start(out=wsT, in_=dram["wsT", l]), writes=["wsT"], dma_key="wsT")
        cw = pp[:, 0:124].rearrange("p (c j) -> p c j", c=4)
        cb_ = pp[:, 124:128]
        clng = pp[:, 128:132]
        clnb = pp[:, 132:136]
        pwb = pp[:, 136:140]

        p0_pro, p0_front, p0_back, _fl = make_p0(l, x_src)
        nh = p0_hoisted.get(l, 0)
        if nh == 0:
            p0_pro()
            p0_front(0)
            nh = 1
        for i in range(NTK):
            if i + 1 >= nh and i + 1 < NTK:
                p0_front(i + 1)
            p0_back(i)

        fence()
        cv = Carver()
        KT = [cv.bf16(TKMAX), cv.bf16(TKMAX)]
        QT = [cv.bf16(TFMAX), cv.bf16(TFMAX)]
        GT = [cv.bf16(TFMAX), cv.bf16(TFMAX)]
        vT = [cv.bf16(TKMAX), cv.bf16(TKMAX)]
        Vh = [cv.bf16(12 * 130), cv.bf16(12 * 130)]
        biasb = [cv.f32(1920), cv.f32(1920)]
        sbS = [cv.f32(640), cv.f32(640)]
        PT = [cv.bf16(640), cv.bf16(640)]
        an = [cv.bf16(128), cv.bf16(128)]
        th = cv.f32(512)
        rc = stat2
        for pb in range(2):
            P.add("dve", (lambda a: lambda e: e.memset(a, 2.0))(Vh[pb]), writes=[("ar", "Vh", pb)])

        def proj_fm(l, tag, T, evac, nb=2):
            wap, wkeys, wi = w_use(l, tag)
            for (t0, n) in chunks(T):
                b = next_bank(nb)
                for k in range(16):
                    P.add("pe", MM(PS[:, b * 512:b * 512 + n], wap[:, k * 128:(k + 1) * 128], hTs(k, t0, n), k == 0, k == 15),
                          reads=wkeys + tile_keys(t0, n), writes=psq(b * 512, n))
                evac(b, t0, n)
                yield
            w_release(wi)

        def gen_proj(h):
            pb = h % 2

            def ev_v(b, t0, n):
                P.add("dve", CP(vT[pb][:, t0:t0 + n], PS[:, b * 512:b * 512 + n]), reads=psq(b * 512, n),
                      writes=[("ar", "vT", pb)])

            def ev_k(b, t0, n):
                P.add("act", (lambda d, s: lambda e: e.copy(out=d, in_=s))(KT[pb][:, t0:t0 + n], PS[:, b * 512:b * 512 + n]),
                      reads=psq(b * 512, n), writes=[("ar", "KT", pb)])

            def ev_q(b, t0, n):
                P.add("act", ACT(QT[pb][:, t0:t0 + n], PS[:, b * 512:b * 512 + n], AF.Copy, scale=float(128 ** -0.5)),
                      reads=psq(b * 512, n), writes=[("ar", "QT", pb)])

            def ev_g(b, t0, n):
                P.add("act", ACT(th[:, 0:n], PS[:, b * 512:b * 512 + n], AF.Tanh, scale=0.5),
                      reads=psq(b * 512, n), writes=[("ar", "th")])
                P.add("dve", STT(GT[pb][:, t0:t0 + n], th[:, 0:n], 1.0, PS[:, b * 512:b * 512 + n], ALU.add, ALU.mult),
                      reads=psq(b * 512, n) + [("ar", "th")], writes=[("ar", "GT", pb)])

            yield from proj_fm(l, ("v", h), TK, ev_v)
            Vv = Vh[pb].rearrange("p (i c) -> p i c", c=130)
            for g0 in range(0, NTK, 4):
                ng = min(4, NTK - g0)
                slot = (g0 // 4) % 2
                pst = PS[:, 7 * 512 + slot * 256:7 * 512 + slot * 256 + 256].bitcast(BF16)
                for ii in range(ng):
                    i = g0 + ii
                    P.add("pe", TR(pst[:, ii * 128:(ii + 1) * 128], vT[pb][:, i * 128:(i + 1) * 128]),
                          reads=[("ar", "vT", pb), "ident"], writes=psq(7 * 512 + slot * 256, 256))
                P.add("dve", CP(Vv[:, g0:g0 + ng, 0:128], pst[:, 0:ng * 128].rearrange("p (i c) -> p i c", c=128)),
                      reads=psq(7 * 512 + slot * 256, 256) + [("ar", "Vh", pb)], writes=[("ar", "Vh", pb)])
                yield
            yield from proj_fm(l, ("k", h), TK, ev_k)
            yield from proj_fm(l, ("q", h), TF, ev_q)
            yield from proj_fm(l, ("g", h), TF, ev_g)

        def gen_attn(h):
            pb = h % 2
            Vv = Vh[pb].rearrange("p (i c) -> p i c", c=130)
            P.add("sp", (lambda d, s: lambda e: e.dma_start(out=d, in_=s))(biasb[pb], dram["bias", l][h]),
                  writes=[("ar", "bias", pb)], dma_key=("bias", pb))
            SBASE = [2 * 512, 4 * 512]
            OBASE = [3 * 512 + 128, 5 * 512 + 128]

            def st1(m):
                s = m % 2
                ks = max(m - 2, 0)
                for b in range(4 if m < 2 else 5):
                    j = ks + b
                    P.add("pe", MM(PS[:, SBASE[s] + b * 128:SBASE[s] + (b + 1) * 128], KT[pb][:, j * 128:(j + 1) * 128],
                                   QT[pb][:, m * 128:(m + 1) * 128], True, True),
                          reads=[("ar", "KT", pb), ("ar", "QT", pb)], writes=psq(SBASE[s] + b * 128, 128))

            def st2(m):
                s = m % 2
                bs = min(m, 2)
                nc_ = 512 if m < 2 else 640
                P.add("dve", TT(sbS[s][:, 0:nc_], PS[:, SBASE[s]:SBASE[s] + nc_], biasb[pb][:, bs * 640:bs * 640 + nc_], ALU.add),
                      reads=psq(SBASE[s], nc_) + [("ar", "bias", pb)], writes=[("ar", "sbS", s)])
                P.add("act", ACT(PT[s][:, 0:nc_], sbS[s][:, 0:nc_], AF.Exp), reads=[("ar", "sbS", s)], writes=[("ar", "PT", s)])

            def st3(m):
                s = m % 2
                ks = max(m - 2, 0)
                nb_ = 4 if m < 2 else 5
                for b in range(nb_):
                    j = ks + b
                    P.add("pe", MM(PS[:, OBASE[s]:OBASE[s] + 129], PT[s][:, b * 128:(b + 1) * 128], Vv[:, j, 0:129], b == 0, b == nb_ - 1),
                          reads=[("ar", "PT", s), ("ar", "Vh", pb)], writes=psq(OBASE[s], 129))
                P.add("dve", RCP(rc[:, m:m + 1], PS[:, OBASE[s] + 128:OBASE[s] + 129]), reads=psq(OBASE[s], 129),
                      writes=[("rc", m)])
                P.add("act", ACT(an[s], PS[:, OBASE[s]:OBASE[s] + 128], AF.Copy, scale=rc[:, m:m + 1]),
                      reads=psq(OBASE[s], 129) + [("rc", m)], writes=[("ar", "an", s)])

            def st4(m):
                s = m % 2
                grp = (m // 4) % 2
                pst = PS[:, 6 * 512 + grp * 256:6 * 512 + grp * 256 + 256].bitcast(BF16)
                ii = m % 4
                P.add("pe", TR(pst[:, ii * 128:(ii + 1) * 128], an[s]), reads=[("ar", "an", s), "ident"],
                      writes=psq(6 * 512 + grp * 256, 256))
                if ii == 3 or m == NTF - 1:
                    m0 = m - ii
                    nn = (ii + 1) * 128
                    P.add("dve", TT(YTs(h, m0 * 128, nn), pst[:, 0:nn], GT[pb][:, m0 * 128:m0 * 128 + nn], ALU.mult),
                          reads=psq(6 * 512 + grp * 256, 256) + [("ar", "GT", pb)], writes=[("YT", h)])

            for step in range(NTF + 3):
                if step < NTF:
                    st1(step)
                if 0 <= step - 1 < NTF:
                    st2(step - 1)
                if 0 <= step - 2 < NTF:
                    st3(step - 2)
                if 0 <= step - 3 < NTF:
                    st4(step - 3)
                yield

        def interleave(g1, g2):
            a, b = True, True
            while a or b:
                if a:
                    try:
                        next(g1)
                    except StopIteration:
                        a = False
                if b:
                    try:
                        next(g2)
                    except StopIteration:
                        b = False

        def empty():
            return
            yield

        interleave(gen_proj(0), empty())
        for h in range(NH):
            interleave(gen_proj(h + 1) if h + 1 < NH else empty(), gen_attn(h))

        fence()
        cv = Carver()
        gall = cv.f32(NTF * 512)
        vln = cv.bf16(NTF * 512)
        sg = [cv.f32(512), cv.f32(512)]
        t1 = [cv.f32(512), cv.f32(512)]
        yb = [cv.bf16(512), cv.bf16(512)]
        gall3 = gall.rearrange("p (i c) -> p i c", c=512)
        vln3 = vln.rearrange("p (i c) -> p i c", c=512)
        lng_bc = ln512[:, 0:512]
        lnb_bc = ln512[:, 512:1024]

        def proj_tm(tag, evac):
            wap, wkeys, wi = w_use(l, tag)
            for i in range(NTF):
                b = next_bank(4)
                for k in range(16):
                    P.add("pe", MM(PS[:, b * 512:(b + 1) * 512], hTs(k, i * 128, 128), wap[:, k * 512:(k + 1) * 512], k == 0, k == 15),
                          reads=wkeys + [("hT", i)], writes=psq(b * 512, 512))
                evac(b, i)
            w_release(wi)

        def ev_vb(b, i):
            P.add("act", ACT(gall3[:, i, :], PS[:, b * 512:(b + 1) * 512], AF.Gelu), reads=psq(b * 512, 512),
                  writes=[("ar", "gall", i)])
            P.add("dve", (lambda o, s: lambda e: e.bn_stats(out=o, in_=s))(bnst[:, 0:6], gall3[:, i, :]),
                  reads=[("ar", "gall", i)], writes=["bnst"])
            P.add("dve", (lambda o, s: lambda e: e.bn_aggr(out=o, in_=s))(mv[:, 2 * i:2 * i + 2], bnst[:, 0:6]),
                  reads=["bnst"], writes=[("mv", i)])

        proj_tm("vb", ev_vb)
        mv3 = mv[:, 0:2 * NTF].rearrange("p (i c) -> p i c", c=2)
        P.add("act", ACT(stat[:, 0:NTF], mv3[:, :, 1], AF.Sqrt, bias=epsc[:, 0:1], scale=1.0),
              reads=[("mv", i) for i in range(NTF)] + ["epsc", "stat"], writes=["stat"])
        P.add("dve", RCP(stat[:, 16:16 + NTF], stat[:, 0:NTF]), reads=["stat"], writes=["stat"])
        for i in range(NTF):
            P.add("dve", TS(gall3[:, i, :], gall3[:, i, :], mv[:, 2 * i:2 * i + 1], stat[:, 16 + i:17 + i], ALU.subtract, ALU.mult),
                  reads=[("ar", "gall", i), ("mv", i), "stat"], writes=[("ar", "gall", i)])
            P.add("dve", TT(gall3[:, i, :], gall3[:, i, :], lng_bc, ALU.mult), reads=[("ar", "gall", i), "ln512"],
                  writes=[("ar", "gall", i)])
            P.add("dve", TT(vln3[:, i, :], gall3[:, i, :], lnb_bc, ALU.add), reads=[("ar", "gall", i), "ln512"],
                  writes=[("ar", "vln", i)])

        def ev_u(b, i):
            P.add("act", ACT(gall3[:, i, :], PS[:, b * 512:(b + 1) * 512], AF.Gelu), reads=psq(b * 512, 512),
                  writes=[("ar", "gall", i)])

        proj_tm("u", ev_u)

        def ev_gB(b, i):
            s = i % 2
            P.add("act", ACT(sg[s], PS[:, b * 512:(b + 1) * 512], AF.Silu), reads=psq(b * 512, 512),
                  writes=[("ar", "sg", s)])
            sbk = 4 + s
            for g in range(4):
                P.add("pe", MM(PS[:, sbk * 512 + g * 128:sbk * 512 + (g + 1) * 128], wsT[:, g * 128:(g + 1) * 128],
                               vln3[:, i, g * 128:(g + 1) * 128], True, True),
                      reads=["wsT", ("ar", "vln", i)], writes=psq(sbk * 512 + g * 128, 128))
            P.add("dve", TT(t1[s], PS[:, sbk * 512:(sbk + 1) * 512], bsb, ALU.add), reads=psq(sbk * 512, 512) + ["bsb"],
                  writes=[("ar", "t1", s)])
            P.add("dve", TT(t1[s], t1[s], gall3[:, i, :], ALU.mult), reads=[("ar", "t1", s), ("ar", "gall", i)],
                  writes=[("ar", "t1", s)])
            P.add("dve", TT(yb[s], t1[s], sg[s], ALU.mult), reads=[("ar", "t1", s), ("ar", "sg", s)],
                  writes=[("ar", "yb", s)])
            if i > 0:
                gB_back(i - 1)

        def gB_back(i):
            s = i % 2
            tb = (6 + s) * 512
            pst = PS[:, tb:tb + 256].bitcast(BF16)
            for g in range(4):
                P.add("pe", TR(pst[:, g * 128:(g + 1) * 128], yb[s][:, g * 128:(g + 1) * 128]),
                      reads=[("ar", "yb", s), "ident"], writes=psq(tb, 256))
            P.add("act", (lambda d, s_: lambda e: e.copy(out=d, in_=s_))(YT3[:, 8:12, i * 128:(i + 1) * 128],
                                                                      pst.rearrange("p (g t) -> p g t", g=4)),
                  reads=psq(tb, 256), writes=[("YT", 8 + g) for g in range(4)])

        proj_tm("gB", ev_gB)
        gB_back(NTF - 1)

        fence()
        cv = Carver()
        TC = TF + 16
        P.add("pool", (lambda d, s_: lambda e: e.dma_start(out=d, in_=s_))(ln512.bitcast(BF16), dram["wst", l][56]),
              writes=["ln512"], dma_key="pw")
        acc = cv.f32(4 * TFMAX)
        hcb = [cv.bf16(TFMAX + 32), cv.bf16(TFMAX + 32)]
        Dgs = [cv.bf16(31 * 128), cv.bf16(31 * 128)]
        sgm = cv.f32(512)
        sqb = [cv.f32(512), cv.f32(512)]
        mean_sb = cv.f32(TFMAX)
        tmpc = cv.f32(TFMAX)
        hn = cv.bf16(4 * TFMAX)
        accd = cv.f32(TFMAX)
        JP = 24
        pending = []

        def drain(k):
            for _ in range(min(k, len(pending))):
                pending.pop(0)()
        ident_b = ident.unsqueeze(1).to_broadcast([128, 31, 128])
        for pbh in range(2):
            P.add("dve", (lambda a: lambda e: e.memset(a, 0.0))(hcb[pbh][:, 0:16]), writes=[("ar", "hcb", pbh)])
        for ct in range(4):
            hb = hcb[ct % 2]
            hkey = ("ar", "hcb", ct % 2)
            Dg = Dgs[ct % 2]
            Dg3 = Dg.rearrange("p (j c) -> p j c", j=31)
            dkey = ("ar", "Dg", ct % 2)
            P.add("dve", TT(Dg3, ident_b, cw[:, ct, :].unsqueeze(2).to_broadcast([128, 31, 128]), ALU.mult),
                  reads=["ident", "pp"], writes=[dkey])
            wa, ka, ia = w_use(l, ("a", ct))
            wb, kb, ib = w_use(l, ("b", ct))
            for (t0, n) in bchunks(TC):
                ba = (0, 1, 6, 7)[next_bank(4)]
                for k in range(16):
                    P.add("pe", MM(PS[:, ba * 512:ba * 512 + n], wa[:, k * 128:(k + 1) * 128], hTs(k, t0, n), k == 0, k == 15),
                          reads=ka + tile_keys(t0, n), writes=psq(ba * 512, n))
                bb = (0, 1, 6, 7)[next_bank(4)]
                for k in range(16):
                    P.add("pe", MM(PS[:, bb * 512:bb * 512 + n], wb[:, k * 128:(k + 1) * 128], hTs(k, t0, n), k == 0, k == 15),
                          reads=kb + tile_keys(t0, n), writes=psq(bb * 512, n))
                drain(3)
                P.add("act", ACT(sgm[:, 0:n], PS[:, bb * 512:bb * 512 + n], AF.Sigmoid), reads=psq(bb * 512, n),
                      writes=[("ar", "sgm")])
                P.add("dve", TT(hb[:, 15 + t0:15 + t0 + n], PS[:, ba * 512:ba * 512 + n], sgm[:, 0:n], ALU.mult),
                      reads=psq(ba * 512, n) + [("ar", "sgm"), hkey], writes=[hkey])
            w_release(ia)
            w_release(ib)
            drain(99)
            a_ct = acc[:, ct * TF:(ct + 1) * TF]
            for ci, (t0, n) in enumerate(chunks(TF)):
                bk = 2 + (ci % 2)
                for j in range(JP):
                    P.add("pe", MM(PS[:, bk * 512:bk * 512 + n], Dg[:, j * 128:(j + 1) * 128], hb[:, t0 + j:t0 + j + n], j == 0, j == JP - 1),
                          reads=[dkey, hkey], writes=psq(bk * 512, n))
                P.add("act", ACT(a_ct[:, t0:t0 + n], PS[:, bk * 512:bk * 512 + n], AF.Identity, bias=cb_[:, ct:ct + 1]),
                      reads=psq(bk * 512, n) + ["pp"], writes=[("ar", "acc", ct, ci)])
            def tap_op(j, ct=ct, hb=hb, hkey=hkey):
                if j == JP:
                    P.add("dve", TS(accd[:, 0:TF], hb[:, j:j + TF], cw[:, ct, j:j + 1], None, ALU.mult),
                          reads=[hkey, "pp"], writes=[("ar", "accd")])
                else:
                    P.add("dve", STT(accd[:, 0:TF], hb[:, j:j + TF], cw[:, ct, j:j + 1], accd[:, 0:TF], ALU.mult, ALU.add),
                          reads=[hkey, "pp", ("ar", "accd")], writes=[("ar", "accd")])

            def comb_op(ct=ct, a_ct=a_ct):
                nchk = len(chunks(TF))
                P.add("dve", TT(a_ct, a_ct, accd[:, 0:TF], ALU.add),
                      reads=[("ar", "acc", ct, ci) for ci in range(nchk)] + [("ar", "accd")],
                      writes=[("ar", "acc", ct, ci) for ci in range(nchk)])

            for j in range(JP, 31):
                pending.append((lambda j=j, f=tap_op: f(j)))
            pending.append(comb_op)
        drain(99)
        def gen_ln():
            chs = chunks(TF)
            banks = [(4, 5), (6, 7), (2, 3)]
            for ci, (t0, n) in enumerate(chs):
                ac = [acc[:, ct * TF + t0:ct * TF + t0 + n] for ct in range(4)]
                ms = mean_sb[:, t0:t0 + n]
                vs = tmpc[:, t0:t0 + n]
                P.add("dve", TT(ms, ac[0], ac[1], ALU.add), reads=[("ar", "acc", 0, ci), ("ar", "acc", 1, ci)],
                      writes=[("ar", "mean", ci)])
                P.add("act", ACT(vs, ac[0], AF.Square), reads=[("ar", "acc", 0, ci)], writes=[("ar", "tmpc", ci)])
                for ct in range(1, 4):
                    if ct >= 2:
                        P.add("dve", TT(ms, ms, ac[ct], ALU.add), reads=[("ar", "mean", ci), ("ar", "acc", ct, ci)],
                              writes=[("ar", "mean", ci)])
                    sb_ = sqb[ct % 2]
                    P.add("act", ACT(sb_[:, 0:n], ac[ct], AF.Square), reads=[("ar", "acc", ct, ci)], writes=[("ar", "sqb", ct % 2)])
                    P.add("dve", TT(vs, vs, sb_[:, 0:n], ALU.add), reads=[("ar", "tmpc", ci), ("ar", "sqb", ct % 2)],
                          writes=[("ar", "tmpc", ci)])
                yield
            for ci, (t0, n) in enumerate(chs):
                bm, bx = banks[ci]
                P.add("pe", MM(PS[:, bm * 512:bm * 512 + n], onesf, mean_sb[:, t0:t0 + n], True, True),
                      reads=["onesf", ("ar", "mean", ci)], writes=psq(bm * 512, n))
                P.add("pe", MM(PS[:, bx * 512:bx * 512 + n], onesf, tmpc[:, t0:t0 + n], True, True),
                      reads=["onesf", ("ar", "tmpc", ci)], writes=psq(bx * 512, n))
            yield
            for ci, (t0, n) in enumerate(chs):
                bm, bx = banks[ci]
                P.add("act", (lambda d, s: lambda e: e.copy(out=d, in_=s))(mean_sb[:, t0:t0 + n], PS[:, bm * 512:bm * 512 + n]),
                      reads=psq(bm * 512, n), writes=[("ar", "mean", ci)])
                P.add("dve", TT(sgm[:, 0:n], mean_sb[:, t0:t0 + n], mean_sb[:, t0:t0 + n], ALU.mult), reads=[("ar", "mean", ci)],
                      writes=[("ar", "sgm")])
                P.add("dve", TT(tmpc[:, t0:t0 + n], PS[:, bx * 512:bx * 512 + n], sgm[:, 0:n], ALU.subtract),
                      reads=psq(bx * 512, n) + [("ar", "sgm")], writes=[("ar", "tmpc", ci)])
                yield
            nch = len(chs)
            P.add("act", ACT(tmpc[:, 0:TF], tmpc[:, 0:TF], AF.Sqrt, bias=epsc[:, 0:1], scale=1.0),
                  reads=[("ar", "tmpc", ci) for ci in range(nch)] + ["epsc"], writes=[("ar", "tmpc", ci) for ci in range(nch)])
            yield
            for ci, (t0, n) in enumerate(chs):
                P.add("dve", RCP(tmpc[:, t0:t0 + n], tmpc[:, t0:t0 + n]), reads=[("ar", "tmpc", ci)], writes=[("ar", "tmpc", ci)])
                for ct in range(4):
                    a_c = acc[:, ct * TF + t0:ct * TF + t0 + n]
                    P.add("dve", TT(a_c, a_c, mean_sb[:, t0:t0 + n], ALU.subtract), reads=[("ar", "acc", ct, ci), ("ar", "mean", ci)],
                          writes=[("ar", "acc", ct, ci)])
                    P.add("dve", TT(a_c, a_c, tmpc[:, t0:t0 + n], ALU.mult), reads=[("ar", "acc", ct, ci), ("ar", "tmpc", ci)],
                          writes=[("ar", "acc", ct, ci)])
                    P.add("act", ACT(hn[:, ct * TF + t0:ct * TF + t0 + n], a_c, AF.Silu, bias=clnb[:, ct:ct + 1], scale=clng[:, ct:ct + 1]),
                          reads=[("ar", "acc", ct, ci), "pp"], writes=[("ar", "hn", ct, ci)])
                    if ct % 2 == 1:
                        yield


        def gen_gc():
            for ct in range(4):
                def ev_gc(b, t0, n, ct=ct):
                    P.add("act", ACT(YTs(12 + ct, t0, n), PS[:, b * 512:b * 512 + n], AF.Silu), reads=psq(b * 512, n),
                          writes=[("YT", 12 + ct)])

                yield from proj_fm(l, ("gC", ct), TF, ev_gc)

        interleave(gen_gc(), gen_ln())
        wpw = ln512.bitcast(BF16)
        kpw = ["ln512"]
        for ct in range(4):
            for (t0, n) in chunks(TF):
                b = next_bank(2)
                for k in range(4):
                    P.add("pe", MM(PS[:, b * 512:b * 512 + n], wpw[:, k * 512 + ct * 128:k * 512 + (ct + 1) * 128],
                                   hn[:, k * TF + t0:k * TF + t0 + n], k == 0, k == 3),
                          reads=kpw + [("ar", "hn", k, t0 // 512)], writes=psq(b * 512, n))
                P.add("dve", STT(YTs(12 + ct, t0, n), PS[:, b * 512:b * 512 + n], pwb[:, ct:ct + 1], YTs(12 + ct, t0, n), ALU.add, ALU.mult),
                      reads=psq(b * 512, n) + ["pp", ("YT", 12 + ct)], writes=[("YT", 12 + ct)])

        fence()
        NX = 3
        P.add("sp", (lambda d, s: lambda e: e.dma_start(out=d, in_=s))(gpost_b, dram["gpost", l].partition_broadcast(128)),
              writes=[("ar", "gpost")], dma_key="gpost")
        P.add("dve", lambda e: e.memset(stat, 0.0), writes=["stat"] + [("stat", i) for i in range(64)], reads=["stat"])
        G = (NTF + 1) // 2
        groups = [list(range(0, G)), list(range(G, NTF))]
        ysb = hT
        P.add("dve", lambda e: e.memset(stat2[:, 62:63], 0.0), writes=ALL_HT + ["ysb_claim"])
        xslot = {}
        kept_wo = {}

        xd = [(xinD[k], ("ar", "xinD", k)) for k in range(3)]
        if (l + 1) not in layer_ids:
            xd += [(xin0[k], ("ar", "xin0", k)) for k in range(2)]
        NX = len(xd)

        def emit_xload(t):
            sl = len(xslot) % NX
            xslot[t] = sl
            P.add("sp", (lambda d, s: lambda e: e.dma_start(out=d, in_=s))(xd[sl][0], x_src[t * 128:(t + 1) * 128, :]),
                  reads=[("xsrc", l, t)], writes=[xd[sl][1]], dma_key=xd[sl][1])

        for gi, grp in enumerate(groups):
            for t in grp[:NX]:
                emit_xload(t)
            nslab = (16 * TK * 2) // 8192
            ysl = ysb[:, 0:2 * 2048 * nslab].bitcast(F32)
            sl_of = [(ti + gi * len(groups[0])) % nslab for ti in range(len(grp))]
            for cidx, cbk in enumerate((0, 1, 2, 3) if gi == 0 else (3, 2, 1, 0)):
                if gi == 1 and cbk >= 2:
                    wo, ko, io = kept_wo[cbk]
                else:
                    wo, ko, io = w_use(l, ("wo", gi, cbk))
                for ti, t in enumerate(grp):
                    b = next_bank(4)
                    for k in range(16):
                        P.add("pe", MM(PS[:, b * 512:(b + 1) * 512], YTs(k, t * 128, 128), wo[:, k * 512:(k + 1) * 512], k == 0, k == 15),
                              reads=ko + [("YT", k)], writes=psq(b * 512, 512))
                    ys = sl_of[ti]
                    ydst = ysl[:, ys * 2048 + cbk * 512:ys * 2048 + (cbk + 1) * 512]
                    P.add("act", ACT(junk0[:, 0:512], PS[:, b * 512:(b + 1) * 512], AF.Square, accum_out=stat[:, 4 * t + cbk:4 * t + cbk + 1]),
                          reads=psq(b * 512, 512) + ["stat"], writes=[("ar", "junk"), ("stat", 4 * t + cbk)])
                    P.add("act", (lambda d, s_: lambda e: e.copy(out=d, in_=s_))(ydst, PS[:, b * 512:(b + 1) * 512]),
                          reads=psq(b * 512, 512) + ["ysb_claim"], writes=[("ysb", ys, cbk)])
                    if cidx == 3:
                        sl = xslot[t]
                        st4_ = stat[:, 4 * t:4 * t + 4]
                        P.add("dve", (lambda o, s_: lambda e: e.tensor_reduce(out=o, in_=s_, axis=mybir.AxisListType.X, op=ALU.add))(stat2[:, t:t + 1], st4_),
                              reads=[("stat", 4 * t + c) for c in range(4)], writes=[("s2", t)])
                        P.add("act", ACT(stat2[:, 16 + t:17 + t], stat2[:, t:t + 1], AF.Sqrt, bias=epsc[:, 0:1], scale=1.0 / D),
                              reads=[("s2", t), "epsc"], writes=[("s2", 16 + t)])
                        P.add("dve", RCP(stat2[:, 32 + t:33 + t], stat2[:, 16 + t:17 + t]), reads=[("s2", 16 + t)],
                              writes=[("s2", 32 + t)])
                        yt = ysl[:, ys * 2048:(ys + 1) * 2048]
                        P.add("dve", STT(yt, yt, stat2[:, 32 + t:33 + t], gpost_b, ALU.mult, ALU.mult),
                              reads=[("ysb", ys, c) for c in range(4)] + [("s2", 32 + t), ("ar", "gpost")],
                              writes=[("ysb", ys, c) for c in range(4)])
                        P.add("dve", TT(yt, yt, xd[sl][0], ALU.add),
                              reads=[("ysb", ys, c) for c in range(4)] + [xd[sl][1]],
                              writes=[("ysb", ys, c) for c in range(4)])
                        if t < n_dst_tiles:
                            P.add("sp", (lambda d, s_: lambda e: e.dma_start(out=d, in_=s_))(x_dst[t * 128:(t + 1) * 128, :], yt),
                                  reads=[("ysb", ys, c) for c in range(4)], writes=[("xsrc", l + 1, t)], dma_key=("xout", ys))
                        if ti + NX < len(grp):
                            emit_xload(grp[ti + NX])
                if gi == 0 and cbk >= 2:
                    kept_wo[cbk] = (wo, ko, io)
                else:
                    w_release(io)
                if gi == 1 and cidx == 1 and (l + 1) in layer_ids:
                    nfront(0, load=False)
                    nfront(1, load=False)
            if gi == 0 and (l + 1) in layer_ids:
                npro, nfront, _, nload = make_p0(l + 1, x_dst)
                npro()
                nload(0)
                nload(1)
                p0_hoisted[l + 1] = 2
        P.add("dve", lambda e: e.memset(stat2[:, 61:62], 0.0),
              writes=ALL_HT + [("ysb", ti, c) for ti in range(8) for c in range(4)])

    with nc.allow_low_precision("bf16 matmul operands, fp32 accumulation"):
        if len(layer_ids) == 2:
            emit_layer(0, dram["x_in"], dram["x_mid"], LAYER_TILES[1][1])
            emit_layer(1, dram["x_mid"], dram["x_out"], 8)
            final_keys = [("xsrc", 2, t) for t in range(8)]
        else:
            l = layer_ids[0]
            emit_layer(l, dram["x_in"], dram["x_out"], ext_out_tiles)
            final_keys = [("xsrc", l + 1, t) for t in range(ext_out_tiles)]
        P.add("sp", None, reads=final_keys)
        P.emit()
    return nc, P


def _small(w, c0):
    blk = w[:, c0:c0 + 128]
    return blk.reshape(16, 128, 128).transpose(1, 0, 2).reshape(128, 2048)


def _big(w, c0):
    blk = w[:, c0:c0 + 512]
    b = blk.reshape(16, 128, 512).transpose(1, 0, 2).reshape(128, 8192)
    return b.reshape(128, 4, 2048).transpose(1, 0, 2)


def prep_wst(w_in, pw, w_out):
    wst = np.empty((NSLOT_DRAM, 128, 2048), np.float32)
    for h in range(NH):
        wst[4 * h + 0] = _small(w_in, 2048 + h * 128)
        wst[4 * h + 1] = _small(w_in, 1024 + h * 128)
        wst[4 * h + 2] = _small(w_in, 0 + h * 128)
        wst[4 * h + 3] = _small(w_in, 3072 + h * 128)
    wst[32:36] = _big(w_in, 4608)
    wst[36:40] = _big(w_in, 4096)
    wst[40:44] = _big(w_in, 5120)
    for ct in range(4):
        wst[44 + 2 * ct] = _small(w_in, 5632 + ct * 128)
        wst[45 + 2 * ct] = _small(w_in, 6144 + ct * 128)
        wst[52 + ct] = _small(w_in, 6656 + ct * 128)
    wst[56] = pw.reshape(4, 128, 512).transpose(1, 0, 2).reshape(128, 2048)
    for cb in range(4):
        wst[57 + 4 * cb:61 + 4 * cb] = _big(w_out, cb * 512)
    return wst


def prep_bias(rpb, half):
    out = np.full((NH, 128, 3, 5, 128), NEG, np.float32)
    loc = np.arange(128)
    for s in range(3):
        tq = s * 128 + loc
        gq = tq if half == 0 else 2047 - tq
        qr, qc = gq // 64, gq % 64
        rs = np.clip(qr - 4, 0, 24)
        cs = np.clip(qc - 8, 0, 48)
        for b in range(5):
            tk = b * 128 + loc
            gk = tk if half == 0 else 2047 - tk
            kr, kc = gk // 64, gk % 64
            KR, QR = kr[:, None], qr[None, :]
            KC, QC = kc[:, None], qc[None, :]
            ok = (KR >= rs[None, :]) & (KR < rs[None, :] + 8) & (KC >= cs[None, :]) & (KC < cs[None, :] + 16)
            dr = np.clip(KR - QR + 7, 0, 14)
            dc = np.clip(KC - QC + 15, 0, 30)
            vals = rpb[:, dr, dc]
            blk = out[:, :, s, b, :]
            blk[:, ok] = vals[:, ok]
    return out.reshape(NH, 128, 1920)


def prep_core_inputs(inp, layer_ids, half):
    m = {}
    for l in layer_ids:
        w = inp["sgu_w"][l]
        wT = w.transpose(0, 2, 1)
        bs = inp["sgu_b"][l]
        cwl = inp["conv_w"][l][:, 0, :]
        if half == 1:
            wT = wT[:, ::-1, ::-1]
            bs = bs[:, ::-1]
            cwl = cwl[::-1]
        m["wsT%d" % l] = np.ascontiguousarray(wT.transpose(1, 0, 2).reshape(128, 512))
        m["bsb%d" % l] = np.ascontiguousarray(np.repeat(bs.T[:, :, None], 128, axis=2).reshape(128, 512))
        ppa = np.empty((128, NPP), np.float32)
        ppa[:, 0:124] = cwl.T.reshape(4, 128, 31).transpose(1, 0, 2).reshape(128, 124)
        ppa[:, 124:128] = inp["conv_b"][l].reshape(4, 128).T
        ppa[:, 128:132] = inp["conv_ln_g"][l].reshape(4, 128).T
        ppa[:, 132:136] = inp["conv_ln_b"][l].reshape(4, 128).T
        ppa[:, 136:140] = inp["conv_pw_b"][l].reshape(4, 128).T
        m["pp%d" % l] = ppa
        m["ln512_%d" % l] = np.concatenate([inp["sgu_ln_g"][l], inp["sgu_ln_b"][l]])[None, :].astype(np.float32)
        m["gpre%d" % l] = np.ascontiguousarray(inp["pre_norm_g"][l][None, :])
        m["gpost%d" % l] = np.ascontiguousarray(inp["post_norm_g"][l][None, :])
        m["bias%d" % l] = prep_bias(inp["attn_rpb"][l], half)
    return m


FUSED = True
_cache = {}


def _get_prog(key, *args):
    if key not in _cache:
        _cache[key] = build_program(*args)[0]
    return _cache[key]


def kernel(**inputs):
    inp = {k: np.asarray(v, dtype=np.float32) for k, v in inputs.items()}
    x = inp["x"]
    wsts = [prep_wst(inp["w_in"][l], inp["conv_pw_w"][l], inp["w_out"][l]) for l in range(2)]
    ntk0 = LAYER_TILES[0][1]
    core_maps = []
    for c in range(8):
        b, half = c // 2, c % 2
        xs = x[b] if half == 0 else x[b, ::-1]
        m = prep_core_inputs(inp, [0, 1], half)
        m["x_in"] = np.ascontiguousarray(xs[:ntk0 * 128])
        m["wst0"], m["wst1"] = wsts[0], wsts[1]
        core_maps.append(m)
    cores = list(range(8))
    if FUSED:
        nc = _get_prog("fused", [0, 1], ntk0, 8)
        res = run_bass_kernel_spmd(nc, core_maps, core_ids=cores)
        outs = [np.asarray(r["x_out"]) for r in res.results]
    else:
        ntk1 = LAYER_TILES[1][1]
        ncA = _get_prog("L0", [0], ntk0, ntk1)
        keysA = ["x_in", "wst0", "bias0", "gpre0", "gpost0", "ln512_0", "bsb0", "wsT0", "pp0"]
        resA = run_bass_kernel_spmd(ncA, [{k: m[k] for k in keysA} for m in core_maps], core_ids=cores)
        ncB = _get_prog("L1", [1], ntk1, 8)
        keysB = ["wst1", "bias1", "gpre1", "gpost1", "ln512_1", "bsb1", "wsT1", "pp1"]
        mapsB = []
        for c in range(8):
            mm = {k: core_maps[c][k] for k in keysB}
            mm["x_in"] = np.asarray(resA.results[c]["x_out"])
            mapsB.append(mm)
        resB = run_bass_kernel_spmd(ncB, mapsB, core_ids=cores)
        outs = [np.asarray(r["x_out"]) for r in resB.results]
    out = np.empty((4, 2048, 2048), np.float32)
    for c in range(8):
        b, half = c // 2, c % 2
        if half == 0:
            out[b, 0:1024] = outs[c]
        else:
            out[b, 1024:2048] = outs[c][::-1]
    return out
```

```python
import contextlib
import numpy as np
import concourse.bass as bass
import concourse.mybir as mybir
from concourse.bass_utils import run_bass_kernel_spmd

F32 = mybir.dt.float32
BF16 = mybir.dt.bfloat16
AF = mybir.ActivationFunctionType
ALU = mybir.AluOpType

D = 2048
NH = 8
EPS = 1e-6
NEG = -30000.0
NSLOT_DRAM = 73
NPP = 140
LAYER_TILES = [(10, 12), (8, 10)]
TKMAX = 12 * 128
TFMAX = 10 * 128


class Op:
    __slots__ = ("eng", "fn", "deps", "dma_key", "pos", "need_inc", "milestone", "dma_cnt")

    def __init__(self, eng, fn, deps, dma_key):
        self.eng = eng
        self.fn = fn
        self.deps = deps
        self.dma_key = dma_key
        self.pos = -1
        self.need_inc = False
        self.milestone = 0
        self.dma_cnt = 0


class Prog:
    ENGS = ("pe", "act", "dve", "pool", "sp")

    def __init__(self, nc):
        self.nc = nc
        self.ops = []
        self.res_w = {}
        self.res_r = {}

    def add(self, eng, fn, reads=(), writes=(), dma_key=None):
        reads = list(reads)
        writes = list(writes)
        pbk = [k for k in reads if isinstance(k, tuple) and k[0] == "pb"]
        if pbk:
            reads = [k for k in reads if not (isinstance(k, tuple) and k[0] == "pb")]
            writes = writes + [k for k in pbk if k not in writes]
        if any(isinstance(k, tuple) and k[0] == "ar" for k in reads + writes):
            reads.append("ARENA")
        idx = len(self.ops)
        deps = set()
        for r in reads:
            w = self.res_w.get(r)
            if w is not None:
                deps.add(w)
        for w in writes:
            lw = self.res_w.get(w)
            if lw is not None:
                deps.add(lw)
            for rd in self.res_r.get(w, ()):
                deps.add(rd)
        for r in reads:
            self.res_r.setdefault(r, []).append(idx)
        for w in writes:
            self.res_w[w] = idx
            self.res_r[w] = []
        deps.discard(idx)
        self.ops.append(Op(eng, fn, deps, dma_key))
        return idx

    def emit(self):
        nc = self.nc
        ops = self.ops
        per_eng = {e: [] for e in self.ENGS}
        dma_counts = {}
        for i, op in enumerate(ops):
            op.pos = len(per_eng[op.eng])
            per_eng[op.eng].append(i)
            if op.dma_key is not None:
                dma_counts[op.dma_key] = dma_counts.get(op.dma_key, 0) + 1
                op.dma_cnt = dma_counts[op.dma_key]
        waited = {}
        waits = [None] * len(ops)
        for i, op in enumerate(ops):
            wl = []
            best = {}
            for d in op.deps:
                p = ops[d]
                if p.dma_key is not None:
                    key = ("dma", p.dma_key)
                    val = p.dma_cnt
                else:
                    if p.eng == "pe" and op.eng == "pe" and op.dma_key is None:
                        continue
                    key = ("eng", p.eng)
                    val = p.pos
                if val > best.get(key, (-1, None))[0]:
                    best[key] = (val, d)
            for key, (val, d) in best.items():
                wk = (op.eng, key)
                if waited.get(wk, -1) >= val:
                    continue
                waited[wk] = val
                wl.append(d)
                if ops[d].dma_key is None:
                    ops[d].need_inc = True
            waits[i] = wl
        for e in self.ENGS:
            m = 0
            for i in per_eng[e]:
                op = ops[i]
                if op.dma_key is None and op.need_inc:
                    m += 1
                    op.milestone = m
        self.stats = {e: len(per_eng[e]) for e in self.ENGS}
        with contextlib.ExitStack() as es:
            esem = {e: es.enter_context(nc.semaphore("s_" + e)) for e in self.ENGS}
            dsem = {k: es.enter_context(nc.semaphore("d_%d" % n)) for n, k in enumerate(dma_counts)}
            block = es.enter_context(nc.Block())

            def run(e, eng):
                for i in per_eng[e]:
                    op = ops[i]
                    for d in waits[i]:
                        p = ops[d]
                        if p.dma_key is not None:
                            eng.wait_ge(dsem[p.dma_key], 16 * p.dma_cnt)
                        else:
                            eng.wait_ge(esem[p.eng], p.milestone)
                    if op.fn is None:
                        continue
                    ins = op.fn(eng)
                    if op.dma_key is not None:
                        ins.then_inc(dsem[op.dma_key], 16)
                    elif op.need_inc:
                        ins.then_inc(esem[e], 1)

            @block.tensor
            def _(eng):
                run("pe", eng)

            @block.scalar
            def _(eng):
                run("act", eng)

            @block.vector
            def _(eng):
                run("dve", eng)

            @block.gpsimd
            def _(eng):
                run("pool", eng)

            @block.sync
            def _(eng):
                run("sp", eng)


def chunks(T, step=512):
    out = []
    t = 0
    while t < T:
        n = min(step, T - t)
        out.append((t, n))
        t += n
    return out


def bchunks(T):
    k = (T + 511) // 512
    base = ((T + k - 1) // k + 15) // 16 * 16
    out = []
    t = 0
    while t < T:
        n = min(base, T - t)
        out.append((t, n))
        t += n
    return out


def tile_keys(t0, n):
    return [("hT", i) for i in range(t0 // 128, (t0 + n - 1) // 128 + 1)]


ALL_HT = [("hT", i) for i in range(12)]


def build_program(layer_ids, ext_in_tiles, ext_out_tiles):
    nc = bass.Bass("TRN2", target_bir_lowering=False)
    P = Prog(nc)
    dram = {}
    dram["x_in"] = nc.dram_tensor("x_in", [ext_in_tiles * 128, D], F32, kind="ExternalInput").ap()
    dram["x_out"] = nc.dram_tensor("x_out", [ext_out_tiles * 128, D], F32, kind="ExternalOutput").ap()
    for l in layer_ids:
        dram["wst", l] = nc.dram_tensor("wst%d" % l, [NSLOT_DRAM, 128, 2048], F32, kind="ExternalInput").ap()
        dram["bias", l] = nc.dram_tensor("bias%d" % l, [NH, 128, 1920], F32, kind="ExternalInput").ap()
        dram["gpre", l] = nc.dram_tensor("gpre%d" % l, [1, D], F32, kind="ExternalInput").ap()
        dram["gpost", l] = nc.dram_tensor("gpost%d" % l, [1, D], F32, kind="ExternalInput").ap()
        dram["ln512", l] = nc.dram_tensor("ln512_%d" % l, [1, 1024], F32, kind="ExternalInput").ap()
        dram["bsb", l] = nc.dram_tensor("bsb%d" % l, [128, 512], F32, kind="ExternalInput").ap()
        dram["wsT", l] = nc.dram_tensor("wsT%d" % l, [128, 512], F32, kind="ExternalInput").ap()
        dram["pp", l] = nc.dram_tensor("pp%d" % l, [128, NPP], F32, kind="ExternalInput").ap()
    if len(layer_ids) == 2:
        dram["x_mid"] = nc.dram_tensor("x_mid", [LAYER_TILES[1][1] * 128, D], F32).ap()

    def sb(name, shape, dt):
        return nc.alloc_sbuf_tensor(name, shape, dt).ap()

    hT = sb("hT", [128, 16 * TKMAX], BF16)
    YT = sb("YT", [128, 16 * TFMAX], BF16)
    ring = sb("ring", [128, 8 * 2048], BF16)
    ident = sb("ident", [128, 128], BF16)
    identf = sb("identf", [128, 128], F32)
    onesf = sb("onesf", [128, 128], F32)
    epsc = sb("epsc", [128, 1], F32)
    pp = sb("pp", [128, NPP], F32)
    ln512 = sb("ln512", [128, 1024], F32)
    bsb = sb("bsb", [128, 512], F32)
    wsT = sb("wsT", [128, 512], BF16)
    stat = sb("stat", [128, 64], F32)
    stat2 = sb("stat2", [128, 64], F32)
    bnst = sb("bnst", [128, 8], F32)
    mv = sb("mv", [128, 24], F32)
    ARENA_F32 = 18688
    arena = sb("arena", [128, ARENA_F32], F32)
    PS = nc.alloc_psum_tensor("ps", [128, 4096], F32).ap()

    class Carver:
        def __init__(self):
            self.off = 0

        def f32(self, n):
            a = arena[:, self.off:self.off + n]
            self.off += n
            assert self.off <= ARENA_F32, self.off
            return a

        def bf16(self, n):
            assert n % 2 == 0
            a = arena[:, self.off:self.off + n // 2].bitcast(BF16)
            self.off += n // 2
            assert self.off <= ARENA_F32, self.off
            return a

    def psq(c0, n):
        return [("pb", q) for q in range(c0 // 512, (c0 + n - 1) // 512 + 1)]

    def fence():
        P.add("dve", lambda e: e.memset(stat2[:, 63:64], 0.0), writes=["ARENA", "fence_cell"])

    P.add("pool", lambda e: e.memset(identf, 0.0), writes=["identf"])
    P.add("pool", lambda e: e.affine_select(out=identf, in_=identf, pattern=[[-1, 128]], compare_op=ALU.not_equal,
                                            fill=1.0, base=0, channel_multiplier=1), reads=["identf"], writes=["identf"])
    P.add("dve", lambda e: e.tensor_copy(out=ident, in_=identf), reads=["identf"], writes=["ident"])
    P.add("dve", lambda e: e.memset(onesf, 1.0 / 512.0), writes=["onesf"])
    P.add("dve", lambda e: e.memset(epsc, EPS), writes=["epsc"])

    items = []
    item_index = {}

    def add_item(l, tag, slot, n):
        item_index.setdefault((l, tag), []).append(len(items))
        items.append((l, slot, n))

    for l in layer_ids:
        ntf = LAYER_TILES[l][0]
        for h in range(NH):
            for j, nm in enumerate(("v", "k", "q", "g")):
                add_item(l, (nm, h), 4 * h + j, 1)
        add_item(l, "vb", 32, 4)
        add_item(l, "u", 36, 4)
        add_item(l, "gB", 40, 4)
        for ct in range(4):
            add_item(l, ("a", ct), 44 + 2 * ct, 1)
            add_item(l, ("b", ct), 45 + 2 * ct, 1)
        for ct in range(4):
            add_item(l, ("gC", ct), 52 + ct, 1)
        for cb in range(4):
            add_item(l, ("wo", 0, cb), 57 + 4 * cb, 4)
        for cb in (1, 0):
            add_item(l, ("wo", 1, cb), 57 + 4 * cb, 4)

    ring_slot = []
    pos = 0
    for (l, slot, n) in items:
        if n == 4:
            pos = (pos + 3) // 4 * 4
        ring_slot.append(pos % 8)
        pos += n
    occ = [None] * 8
    wstate = {"next": 0, "limit": 4, "extra": []}

    def pump():
        while wstate["next"] < len(items):
            i = wstate["next"]
            if wstate.get("limit") is not None and i >= wstate["limit"]:
                return
            l, slot, n = items[i]
            rs = ring_slot[i]
            if any(occ[rs + q] is not None for q in range(n)):
                return
            for q in range(n):
                occ[rs + q] = i
            dst = ring[:, rs * 2048:(rs + n) * 2048]
            if n == 1:
                src = dram["wst", l][slot]
            else:
                dst = dst.rearrange("p (s f) -> p s f", s=n)
                src = dram["wst", l][slot:slot + n].rearrange("s p f -> p s f")
            P.add("pool", (lambda d, s: lambda e: e.dma_start(out=d, in_=s))(dst, src), reads=list(wstate["extra"]),
                  writes=[("ring", rs + q) for q in range(n)], dma_key=("ring", rs))
            wstate["next"] += 1

    def w_use(l, tag, k=0):
        i = item_index[(l, tag)][k]
        assert i < wstate["next"], ("weight item not issued", l, tag)
        rs = ring_slot[i]
        n = items[i][2]
        return ring[:, rs * 2048:(rs + n) * 2048], [("ring", rs + q) for q in range(n)], i

    def w_release(i):
        n = items[i][2]
        rs = ring_slot[i]
        for q in range(n):
            assert occ[rs + q] == i
            occ[rs + q] = None
        pump()

    pump()

    def MM(out, lhsT, rhs, start, stop):
        return lambda e: e.matmul(out, lhsT=lhsT, rhs=rhs, start=start, stop=stop)

    def TR(out, in_):
        return lambda e: e.transpose(out=out, in_=in_, identity=ident)

    def ACT(out, in_, func, bias=None, scale=None, accum_out=None):
        kw = {}
        if bias is not None:
            kw["bias"] = bias
        if scale is not None:
            kw["scale"] = scale
        if accum_out is not None:
            kw["accum_out"] = accum_out
        return lambda e: e.activation(out=out, in_=in_, func=func, **kw)

    def TT(out, in0, in1, op):
        return lambda e: e.tensor_tensor(out=out, in0=in0, in1=in1, op=op)

    def TS(out, in0, s1, s2, op0, op1=None):
        if op1 is None:
            return lambda e: e.tensor_scalar(out=out, in0=in0, scalar1=s1, scalar2=None, op0=op0)
        return lambda e: e.tensor_scalar(out=out, in0=in0, scalar1=s1, scalar2=s2, op0=op0, op1=op1)

    def STT(out, in0, scalar, in1, op0, op1):
        return lambda e: e.scalar_tensor_tensor(out=out, in0=in0, scalar=scalar, in1=in1, op0=op0, op1=op1)

    def CP(out, in_):
        return lambda e: e.tensor_copy(out=out, in_=in_)

    def RCP(out, in_):
        return lambda e: e.reciprocal(out=out, in_=in_)

    main_bank = {"i": 0}

    def next_bank(nb):
        b = main_bank["i"] % nb
        main_bank["i"] += 1
        return b

    cv0 = Carver()
    xinD = [cv0.f32(2048) for _ in range(3)]
    xin0 = [cv0.f32(2048) for _ in range(2)]
    junk0 = cv0.bf16(2048)
    gpost_b = cv0.f32(2048)
    gpre_b = cv0.f32(2048)
    xn0 = [cv0.bf16(2048), cv0.bf16(2048)]
    stat0 = sb("stat0", [128, 64], F32)
    xbufs = [(xin0[0], ("ar", "xin0", 0)), (xin0[1], ("ar", "xin0", 1)), (xinD[0], ("ar", "xinD", 0)),
             (xinD[1], ("ar", "xinD", 1)), (xinD[2], ("ar", "xinD", 2))]

    def make_p0(l, x_src):
        NTK = LAYER_TILES[l][1]
        TK = NTK * 128
        hT3 = hT[:, 0:16 * TK].rearrange("p (k t) -> p k t", k=16)

        def prologue():
            P.add("sp", (lambda d, s: lambda e: e.dma_start(out=d, in_=s))(gpre_b, dram["gpre", l].partition_broadcast(128)),
                  writes=[("ar", "gpre")], dma_key="gpre")
            P.add("dve", lambda e: e.memset(stat0, 0.0), writes=["stat0"] + [("stat0", i) for i in range(64)], reads=["stat0"])

        def front_load(i):
            xbuf, xkey = xbufs[i % 5]
            P.add("sp", (lambda d, s: lambda e: e.dma_start(out=d, in_=s))(xbuf, x_src[i * 128:(i + 1) * 128, :]),
                  reads=[("xsrc", l, i)], writes=[xkey], dma_key=xkey)

        def front(i, load=True):
            pb = i % 2
            xbuf, xkey = xbufs[i % 5]
            if load:
                front_load(i)
            P.add("act", ACT(junk0, xbuf, AF.Square, accum_out=stat0[:, i:i + 1]),
                  reads=[xkey, "stat0"], writes=[("ar", "junk"), ("stat0", i)])
            P.add("act", ACT(stat0[:, 16 + i:17 + i], stat0[:, i:i + 1], AF.Sqrt, bias=epsc[:, 0:1], scale=1.0 / D),
                  reads=[("stat0", i), "epsc", "stat0"], writes=[("stat0", 16 + i)])
            P.add("dve", RCP(stat0[:, 32 + i:33 + i], stat0[:, 16 + i:17 + i]),
                  reads=[("stat0", 16 + i)], writes=[("stat0", 32 + i)])
            P.add("dve", STT(xn0[pb], xbuf, stat0[:, 32 + i:33 + i], gpre_b, ALU.mult, ALU.mult),
                  reads=[xkey, ("stat0", 32 + i), ("ar", "gpre")], writes=[("ar", "xn", pb)])

        def back(i):
            pb = i % 2
            for half in range(2):
                bank = 2 + 2 * pb + half
                pst = PS[:, bank * 512:(bank + 1) * 512].bitcast(BF16)
                for kk in range(8):
                    k = half * 8 + kk
                    P.add("pe", TR(pst[:, kk * 128:(kk + 1) * 128], xn0[pb][:, k * 128:(k + 1) * 128]),
                          reads=[("ar", "xn", pb), "ident"], writes=psq(bank * 512, 512))
                dst = hT3[:, half * 8:(half + 1) * 8, i * 128:(i + 1) * 128]
                srcv = pst.rearrange("p (k t) -> p k t", k=8)
                if half == 0:
                    P.add("act", (lambda d, s: lambda e: e.copy(out=d, in_=s))(dst, srcv),
                          reads=psq(bank * 512, 512), writes=[("hT", i)])
                else:
                    P.add("dve", CP(dst, srcv), reads=psq(bank * 512, 512) + [("hT", i)], writes=[("hT", i)])

        return prologue, front, back, front_load

    p0_hoisted = {}

    def emit_layer(l, x_src, x_dst, n_dst_tiles):
        NTF, NTK = LAYER_TILES[l]
        TF, TK = NTF * 128, NTK * 128
        hT3 = hT[:, 0:16 * TK].rearrange("p (k t) -> p k t", k=16)
        YT3 = YT[:, 0:16 * TF].rearrange("p (k t) -> p k t", k=16)

        def hTs(k, t0, n):
            return hT[:, k * TK + t0:k * TK + t0 + n]

        def YTs(k, t0, n):
            return YT[:, k * TF + t0:k * TF + t0 + n]

        P.add("sp", lambda e: e.dma_start(out=pp, in_=dram["pp", l]), writes=["pp"], dma_key="pp")
        P.add("sp", lambda e: e.dma_start(out=ln512, in_=dram["ln512", l].partition_broadcast(128)), writes=["ln512"], dma_key="ln512")
        P.add("sp", lambda e: e.dma_start(out=bsb, in_=dram["bsb", l]), writes=["bsb"], dma_key="bsb")
        P.add("pool", lambda e: e.dma_start(out=wsT, in_=dram["wsT", l]), writes=["wsT"], dma_key="wsT")
        cw = pp[:, 0:124].rearrange("p (c j) -> p c j", c=4)
        cb_ = pp[:, 124:128]
        clng = pp[:, 128:132]
        clnb = pp[:, 132:136]
        pwb = pp[:, 136:140]

        p0_pro, p0_front, p0_back, _fl = make_p0(l, x_src)
        nh = p0_hoisted.get(l, 0)
        if nh == 0:
            p0_pro()
            p0_front(0)
            nh = 1
        for i in range(NTK):
            if i + 1 >= nh and i + 1 < NTK:
                p0_front(i + 1)
            p0_back(i)
            if i == 3 and wstate.get("limit") is not None:
                wstate["limit"] = None
                wstate["extra"] = [("hT", 3)]
                pump()
                wstate["extra"] = []

        fence()
        cv = Carver()
        KT = [cv.bf16(TKMAX), cv.bf16(TKMAX)]
        QT = [cv.bf16(TFMAX), cv.bf16(TFMAX)]
        GT = [cv.bf16(TFMAX), cv.bf16(TFMAX)]
        vT = [cv.bf16(TKMAX), cv.bf16(TKMAX)]
        Vh = [cv.bf16(12 * 130), cv.bf16(12 * 130)]
        biasb = [cv.f32(1920), cv.f32(1920)]
        sbS = [cv.f32(640), cv.f32(640)]
        PT = [cv.bf16(640), cv.bf16(640)]
        an = [cv.bf16(128), cv.bf16(128)]
        th = cv.f32(512)
        rc = stat2
        for pb in range(2):
            P.add("dve", (lambda a: lambda e: e.memset(a, 2.0))(Vh[pb]), writes=[("ar", "Vh", pb)])

        def proj_fm(l, tag, T, evac, nb=2):
            wap, wkeys, wi = w_use(l, tag)
            for (t0, n) in chunks(T):
                b = next_bank(nb)
                for k in range(16):
                    P.add("pe", MM(PS[:, b * 512:b * 512 + n], wap[:, k * 128:(k + 1) * 128], hTs(k, t0, n), k == 0, k == 15),
                          reads=wkeys + tile_keys(t0, n), writes=psq(b * 512, n))
                evac(b, t0, n)
                yield
            w_release(wi)

        def gen_proj(h):
            pb = h % 2

            def ev_v(b, t0, n):
                P.add("dve", CP(vT[pb][:, t0:t0 + n], PS[:, b * 512:b * 512 + n]), reads=psq(b * 512, n),
                      writes=[("ar", "vT", pb)])

            def ev_k(b, t0, n):
                P.add("act", (lambda d, s: lambda e: e.copy(out=d, in_=s))(KT[pb][:, t0:t0 + n], PS[:, b * 512:b * 512 + n]),
                      reads=psq(b * 512, n), writes=[("ar", "KT", pb)])

            def ev_q(b, t0, n):
                P.add("act", ACT(QT[pb][:, t0:t0 + n], PS[:, b * 512:b * 512 + n], AF.Copy, scale=float(128 ** -0.5)),
                      reads=psq(b * 512, n), writes=[("ar", "QT", pb)])

            def ev_g(b, t0, n):
                P.add("act", ACT(th[:, 0:n], PS[:, b * 512:b * 512 + n], AF.Tanh, scale=0.5),
                      reads=psq(b * 512, n), writes=[("ar", "th")])
                P.add("dve", STT(GT[pb][:, t0:t0 + n], th[:, 0:n], 1.0, PS[:, b * 512:b * 512 + n], ALU.add, ALU.mult),
                      reads=psq(b * 512, n) + [("ar", "th")], writes=[("ar", "GT", pb)])

            yield from proj_fm(l, ("v", h), TK, ev_v)
            Vv = Vh[pb].rearrange("p (i c) -> p i c", c=130)
            for g0 in range(0, NTK, 4):
                ng = min(4, NTK - g0)
                slot = (g0 // 4) % 2
                pst = PS[:, 7 * 512 + slot * 256:7 * 512 + slot * 256 + 256].bitcast(BF16)
                for ii in range(ng):
                    i = g0 + ii
                    P.add("pe", TR(pst[:, ii * 128:(ii + 1) * 128], vT[pb][:, i * 128:(i + 1) * 128]),
                          reads=[("ar", "vT", pb), "ident"], writes=psq(7 * 512 + slot * 256, 256))
                P.add("dve", CP(Vv[:, g0:g0 + ng, 0:128], pst[:, 0:ng * 128].rearrange("p (i c) -> p i c", c=128)),
                      reads=psq(7 * 512 + slot * 256, 256) + [("ar", "Vh", pb)], writes=[("ar", "Vh", pb)])
                yield
            yield from proj_fm(l, ("k", h), TK, ev_k)
            yield from proj_fm(l, ("q", h), TF, ev_q)
            yield from proj_fm(l, ("g", h), TF, ev_g)

        def gen_attn(h):
            pb = h % 2
            Vv = Vh[pb].rearrange("p (i c) -> p i c", c=130)
            P.add("sp", (lambda d, s: lambda e: e.dma_start(out=d, in_=s))(biasb[pb], dram["bias", l][h]),
                  writes=[("ar", "bias", pb)], dma_key=("bias", pb))
            SBASE = [2 * 512, 4 * 512]
            OBASE = [3 * 512 + 128, 5 * 512 + 128]

            def st1(m):
                s = m % 2
                ks = max(m - 2, 0)
                for b in range(4 if m < 2 else 5):
                    j = ks + b
                    P.add("pe", MM(PS[:, SBASE[s] + b * 128:SBASE[s] + (b + 1) * 128], KT[pb][:, j * 128:(j + 1) * 128],
                                   QT[pb][:, m * 128:(m + 1) * 128], True, True),
                          reads=[("ar", "KT", pb), ("ar", "QT", pb)], writes=psq(SBASE[s] + b * 128, 128))

            def st2(m):
                s = m % 2
                bs = min(m, 2)
                nc_ = 512 if m < 2 else 640
                P.add("dve", TT(sbS[s][:, 0:nc_], PS[:, SBASE[s]:SBASE[s] + nc_], biasb[pb][:, bs * 640:bs * 640 + nc_], ALU.add),
                      reads=psq(SBASE[s], nc_) + [("ar", "bias", pb)], writes=[("ar", "sbS", s)])
                P.add("act", ACT(PT[s][:, 0:nc_], sbS[s][:, 0:nc_], AF.Exp), reads=[("ar", "sbS", s)], writes=[("ar", "PT", s)])

            def st3(m):
                s = m % 2
                ks = max(m - 2, 0)
                nb_ = 4 if m < 2 else 5
                for b in range(nb_):
                    j = ks + b
                    P.add("pe", MM(PS[:, OBASE[s]:OBASE[s] + 129], PT[s][:, b * 128:(b + 1) * 128], Vv[:, j, 0:129], b == 0, b == nb_ - 1),
                          reads=[("ar", "PT", s), ("ar", "Vh", pb)], writes=psq(OBASE[s], 129))
                P.add("dve", RCP(rc[:, m:m + 1], PS[:, OBASE[s] + 128:OBASE[s] + 129]), reads=psq(OBASE[s], 129),
                      writes=[("rc", m)])
                P.add("act", ACT(an[s], PS[:, OBASE[s]:OBASE[s] + 128], AF.Copy, scale=rc[:, m:m + 1]),
                      reads=psq(OBASE[s], 129) + [("rc", m)], writes=[("ar", "an", s)])

            def st4(m):
                s = m % 2
                grp = (m // 4) % 2
                pst = PS[:, 6 * 512 + grp * 256:6 * 512 + grp * 256 + 256].bitcast(BF16)
                ii = m % 4
                P.add("pe", TR(pst[:, ii * 128:(ii + 1) * 128], an[s]), reads=[("ar", "an", s), "ident"],
                      writes=psq(6 * 512 + grp * 256, 256))
                if ii == 3 or m == NTF - 1:
                    m0 = m - ii
                    nn = (ii + 1) * 128
                    P.add("dve", TT(YTs(h, m0 * 128, nn), pst[:, 0:nn], GT[pb][:, m0 * 128:m0 * 128 + nn], ALU.mult),
                          reads=psq(6 * 512 + grp * 256, 256) + [("ar", "GT", pb)], writes=[("YT", h)])

            for step in range(NTF + 3):
                if step < NTF:
                    st1(step)
                if 0 <= step - 1 < NTF:
                    st2(step - 1)
                if 0 <= step - 2 < NTF:
                    st3(step - 2)
                if 0 <= step - 3 < NTF:
                    st4(step - 3)
                yield

        def interleave(g1, g2):
            a, b = True, True
            while a or b:
                if a:
                    try:
                        next(g1)
                    except StopIteration:
                        a = False
                if b:
                    try:
                        next(g2)
                    except StopIteration:
                        b = False

        def empty():
            return
            yield

        interleave(gen_proj(0), empty())
        for h in range(NH):
            interleave(gen_proj(h + 1) if h + 1 < NH else empty(), gen_attn(h))

        fence()
        cv = Carver()
        gall = cv.f32(NTF * 512)
        vln = cv.bf16(NTF * 512)
        sg = [cv.f32(512), cv.f32(512)]
        t1 = [cv.f32(512), cv.f32(512)]
        yb = [cv.bf16(512), cv.bf16(512)]
        gall3 = gall.rearrange("p (i c) -> p i c", c=512)
        vln3 = vln.rearrange("p (i c) -> p i c", c=512)
        lng_bc = ln512[:, 0:512]
        lnb_bc = ln512[:, 512:1024]

        def proj_tm(tag, evac):
            wap, wkeys, wi = w_use(l, tag)
            for i in range(NTF):
                b = next_bank(4)
                for k in range(16):
                    P.add("pe", MM(PS[:, b * 512:(b + 1) * 512], hTs(k, i * 128, 128), wap[:, k * 512:(k + 1) * 512], k == 0, k == 15),
                          reads=wkeys + [("hT", i)], writes=psq(b * 512, 512))
                evac(b, i)
            w_release(wi)

        def ev_vb(b, i):
            P.add("act", ACT(gall3[:, i, :], PS[:, b * 512:(b + 1) * 512], AF.Gelu), reads=psq(b * 512, 512),
                  writes=[("ar", "gall", i)])
            P.add("dve", (lambda o, s: lambda e: e.bn_stats(out=o, in_=s))(bnst[:, 0:6], gall3[:, i, :]),
                  reads=[("ar", "gall", i)], writes=["bnst"])
            P.add("dve", (lambda o, s: lambda e: e.bn_aggr(out=o, in_=s))(mv[:, 2 * i:2 * i + 2], bnst[:, 0:6]),
                  reads=["bnst"], writes=[("mv", i)])

        proj_tm("vb", ev_vb)
        mv3 = mv[:, 0:2 * NTF].rearrange("p (i c) -> p i c", c=2)
        P.add("act", ACT(stat[:, 0:NTF], mv3[:, :, 1], AF.Sqrt, bias=epsc[:, 0:1], scale=1.0),
              reads=[("mv", i) for i in range(NTF)] + ["epsc", "stat"], writes=["stat"])
        P.add("dve", RCP(stat[:, 16:16 + NTF], stat[:, 0:NTF]), reads=["stat"], writes=["stat"])
        for i in range(NTF):
            P.add("dve", TS(gall3[:, i, :], gall3[:, i, :], mv[:, 2 * i:2 * i + 1], stat[:, 16 + i:17 + i], ALU.subtract, ALU.mult),
                  reads=[("ar", "gall", i), ("mv", i), "stat"], writes=[("ar", "gall", i)])
            P.add("dve", TT(gall3[:, i, :], gall3[:, i, :], lng_bc, ALU.mult), reads=[("ar", "gall", i), "ln512"],
                  writes=[("ar", "gall", i)])
            P.add("dve", TT(vln3[:, i, :], gall3[:, i, :], lnb_bc, ALU.add), reads=[("ar", "gall", i), "ln512"],
                  writes=[("ar", "vln", i)])

        def ev_u(b, i):
            P.add("act", ACT(gall3[:, i, :], PS[:, b * 512:(b + 1) * 512], AF.Gelu), reads=psq(b * 512, 512),
                  writes=[("ar", "gall", i)])

        proj_tm("u", ev_u)

        def ev_gB(b, i):
            s = i % 2
            P.add("act", ACT(sg[s], PS[:, b * 512:(b + 1) * 512], AF.Silu), reads=psq(b * 512, 512),
                  writes=[("ar", "sg", s)])
            sbk = 4 + s
            for g in range(4):
                P.add("pe", MM(PS[:, sbk * 512 + g * 128:sbk * 512 + (g + 1) * 128], wsT[:, g * 128:(g + 1) * 128],
                               vln3[:, i, g * 128:(g + 1) * 128], True, True),
                      reads=["wsT", ("ar", "vln", i)], writes=psq(sbk * 512 + g * 128, 128))
            P.add("dve", TT(t1[s], PS[:, sbk * 512:(sbk + 1) * 512], bsb, ALU.add), reads=psq(sbk * 512, 512) + ["bsb"],
                  writes=[("ar", "t1", s)])
            P.add("dve", TT(t1[s], t1[s], gall3[:, i, :], ALU.mult), reads=[("ar", "t1", s), ("ar", "gall", i)],
                  writes=[("ar", "t1", s)])
            P.add("dve", TT(yb[s], t1[s], sg[s], ALU.mult), reads=[("ar", "t1", s), ("ar", "sg", s)],
                  writes=[("ar", "yb", s)])
            if i > 0:
                gB_back(i - 1)

        def gB_back(i):
            s = i % 2
            tb = (6 + s) * 512
            pst = PS[:, tb:tb + 256].bitcast(BF16)
            for g in range(4):
                P.add("pe", TR(pst[:, g * 128:(g + 1) * 128], yb[s][:, g * 128:(g + 1) * 128]),
                      reads=[("ar", "yb", s), "ident"], writes=psq(tb, 256))
            P.add("act", (lambda d, s_: lambda e: e.copy(out=d, in_=s_))(YT3[:, 8:12, i * 128:(i + 1) * 128],
                                                                      pst.rearrange("p (g t) -> p g t", g=4)),
                  reads=psq(tb, 256), writes=[("YT", 8 + g) for g in range(4)])

        proj_tm("gB", ev_gB)
        gB_back(NTF - 1)

        fence()
        cv = Carver()
        TC = TF + 16
        P.add("pool", (lambda d, s_: lambda e: e.dma_start(out=d, in_=s_))(ln512.bitcast(BF16), dram["wst", l][56]),
              writes=["ln512"], dma_key="pw")
        acc = cv.f32(4 * TFMAX)
        hcb = [cv.bf16(TFMAX + 32), cv.bf16(TFMAX + 32)]
        Dgs = [cv.bf16(31 * 128), cv.bf16(31 * 128)]
        sgm = cv.f32(512)
        sqb = [cv.f32(512), cv.f32(512)]
        mean_sb = cv.f32(TFMAX)
        tmpc = cv.f32(TFMAX)
        hn = cv.bf16(4 * TFMAX)
        accd = cv.f32(TFMAX)
        JP = 24
        pending = []

        def drain(k):
            for _ in range(min(k, len(pending))):
                pending.pop(0)()
        ident_b = ident.unsqueeze(1).to_broadcast([128, 31, 128])
        for pbh in range(2):
            P.add("dve", (lambda a: lambda e: e.memset(a, 0.0))(hcb[pbh][:, 0:16]), writes=[("ar", "hcb", pbh)])
        for ct in range(4):
            hb = hcb[ct % 2]
            hkey = ("ar", "hcb", ct % 2)
            Dg = Dgs[ct % 2]
            Dg3 = Dg.rearrange("p (j c) -> p j c", j=31)
            dkey = ("ar", "Dg", ct % 2)
            P.add("dve", TT(Dg3, ident_b, cw[:, ct, :].unsqueeze(2).to_broadcast([128, 31, 128]), ALU.mult),
                  reads=["ident", "pp"], writes=[dkey])
            wa, ka, ia = w_use(l, ("a", ct))
            wb, kb, ib = w_use(l, ("b", ct))
            for (t0, n) in bchunks(TC):
                ba = (0, 1, 6, 7)[next_bank(4)]
                for k in range(16):
                    P.add("pe", MM(PS[:, ba * 512:ba * 512 + n], wa[:, k * 128:(k + 1) * 128], hTs(k, t0, n), k == 0, k == 15),
                          reads=ka + tile_keys(t0, n), writes=psq(ba * 512, n))
                bb = (0, 1, 6, 7)[next_bank(4)]
                for k in range(16):
                    P.add("pe", MM(PS[:, bb * 512:bb * 512 + n], wb[:, k * 128:(k + 1) * 128], hTs(k, t0, n), k == 0, k == 15),
                          reads=kb + tile_keys(t0, n), writes=psq(bb * 512, n))
                drain(3)
                P.add("act", ACT(sgm[:, 0:n], PS[:, bb * 512:bb * 512 + n], AF.Sigmoid), reads=psq(bb * 512, n),
                      writes=[("ar", "sgm")])
                P.add("dve", TT(hb[:, 15 + t0:15 + t0 + n], PS[:, ba * 512:ba * 512 + n], sgm[:, 0:n], ALU.mult),
                      reads=psq(ba * 512, n) + [("ar", "sgm"), hkey], writes=[hkey])
            w_release(ia)
            w_release(ib)
            drain(99)
            a_ct = acc[:, ct * TF:(ct + 1) * TF]
            for ci, (t0, n) in enumerate(chunks(TF)):
                bk = 2 + (ci % 2)
                for j in range(JP):
                    P.add("pe", MM(PS[:, bk * 512:bk * 512 + n], Dg[:, j * 128:(j + 1) * 128], hb[:, t0 + j:t0 + j + n], j == 0, j == JP - 1),
                          reads=[dkey, hkey], writes=psq(bk * 512, n))
                P.add("act", ACT(a_ct[:, t0:t0 + n], PS[:, bk * 512:bk * 512 + n], AF.Identity, bias=cb_[:, ct:ct + 1]),
                      reads=psq(bk * 512, n) + ["pp"], writes=[("ar", "acc", ct, ci)])
            def tap_op(j, ct=ct, hb=hb, hkey=hkey):
                if j == JP:
                    P.add("dve", TS(accd[:, 0:TF], hb[:, j:j + TF], cw[:, ct, j:j + 1], None, ALU.mult),
                          reads=[hkey, "pp"], writes=[("ar", "accd")])
                else:
                    P.add("dve", STT(accd[:, 0:TF], hb[:, j:j + TF], cw[:, ct, j:j + 1], accd[:, 0:TF], ALU.mult, ALU.add),
                          reads=[hkey, "pp", ("ar", "accd")], writes=[("ar", "accd")])

            def comb_op(ct=ct, a_ct=a_ct):
                nchk = len(chunks(TF))
                P.add("dve", TT(a_ct, a_ct, accd[:, 0:TF], ALU.add),
                      reads=[("ar", "acc", ct, ci) for ci in range(nchk)] + [("ar", "accd")],
                      writes=[("ar", "acc", ct, ci) for ci in range(nchk)])

            for j in range(JP, 31):
                pending.append((lambda j=j, f=tap_op: f(j)))
            pending.append(comb_op)
        drain(99)
        def gen_ln():
            chs = chunks(TF)
            banks = [(4, 5), (6, 7), (2, 3)]
            for ci, (t0, n) in enumerate(chs):
                ac = [acc[:, ct * TF + t0:ct * TF + t0 + n] for ct in range(4)]
                ms = mean_sb[:, t0:t0 + n]
                vs = tmpc[:, t0:t0 + n]
                P.add("dve", TT(ms, ac[0], ac[1], ALU.add), reads=[("ar", "acc", 0, ci), ("ar", "acc", 1, ci)],
                      writes=[("ar", "mean", ci)])
                P.add("act", ACT(vs, ac[0], AF.Square), reads=[("ar", "acc", 0, ci)], writes=[("ar", "tmpc", ci)])
                for ct in range(1, 4):
                    if ct >= 2:
                        P.add("dve", TT(ms, ms, ac[ct], ALU.add), reads=[("ar", "mean", ci), ("ar", "acc", ct, ci)],
                              writes=[("ar", "mean", ci)])
                    sb_ = sqb[ct % 2]
                    P.add("act", ACT(sb_[:, 0:n], ac[ct], AF.Square), reads=[("ar", "acc", ct, ci)], writes=[("ar", "sqb", ct % 2)])
                    P.add("dve", TT(vs, vs, sb_[:, 0:n], ALU.add), reads=[("ar", "tmpc", ci), ("ar", "sqb", ct % 2)],
                          writes=[("ar", "tmpc", ci)])
                yield
            for ci, (t0, n) in enumerate(chs):
                bm, bx = banks[ci]
                P.add("pe", MM(PS[:, bm * 512:bm * 512 + n], onesf, mean_sb[:, t0:t0 + n], True, True),
                      reads=["onesf", ("ar", "mean", ci)], writes=psq(bm * 512, n))
                P.add("pe", MM(PS[:, bx * 512:bx * 512 + n], onesf, tmpc[:, t0:t0 + n], True, True),
                      reads=["onesf", ("ar", "tmpc", ci)], writes=psq(bx * 512, n))
            yield
            for ci, (t0, n) in enumerate(chs):
                bm, bx = banks[ci]
                P.add("act", (lambda d, s: lambda e: e.copy(out=d, in_=s))(mean_sb[:, t0:t0 + n], PS[:, bm * 512:bm * 512 + n]),
                      reads=psq(bm * 512, n), writes=[("ar", "mean", ci)])
                P.add("dve", TT(sgm[:, 0:n], mean_sb[:, t0:t0 + n], mean_sb[:, t0:t0 + n], ALU.mult), reads=[("ar", "mean", ci)],
                      writes=[("ar", "sgm")])
                P.add("dve", TT(tmpc[:, t0:t0 + n], PS[:, bx * 512:bx * 512 + n], sgm[:, 0:n], ALU.subtract),
                      reads=psq(bx * 512, n) + [("ar", "sgm")], writes=[("ar", "tmpc", ci)])
                yield
            nch = len(chs)
            P.add("act", ACT(tmpc[:, 0:TF], tmpc[:, 0:TF], AF.Sqrt, bias=epsc[:, 0:1], scale=1.0),
                  reads=[("ar", "tmpc", ci) for ci in range(nch)] + ["epsc"], writes=[("ar", "tmpc", ci) for ci in range(nch)])
            yield
            for ci, (t0, n) in enumerate(chs):
                P.add("dve", RCP(tmpc[:, t0:t0 + n], tmpc[:, t0:t0 + n]), reads=[("ar", "tmpc", ci)], writes=[("ar", "tmpc", ci)])
                for ct in range(4):
                    a_c = acc[:, ct * TF + t0:ct * TF + t0 + n]
                    P.add("dve", TT(a_c, a_c, mean_sb[:, t0:t0 + n], ALU.subtract), reads=[("ar", "acc", ct, ci), ("ar", "mean", ci)],
                          writes=[("ar", "acc", ct, ci)])
                    P.add("dve", TT(a_c, a_c, tmpc[:, t0:t0 + n], ALU.mult), reads=[("ar", "acc", ct, ci), ("ar", "tmpc", ci)],
                          writes=[("ar", "acc", ct, ci)])
                    P.add("act", ACT(hn[:, ct * TF + t0:ct * TF + t0 + n], a_c, AF.Silu, bias=clnb[:, ct:ct + 1], scale=clng[:, ct:ct + 1]),
                          reads=[("ar", "acc", ct, ci), "pp"], writes=[("ar", "hn", ct, ci)])
                    if ct % 2 == 1:
                        yield


        def gen_gc():
            for ct in range(4):
                def ev_gc(b, t0, n, ct=ct):
                    P.add("act", ACT(YTs(12 + ct, t0, n), PS[:, b * 512:b * 512 + n], AF.Silu), reads=psq(b * 512, n),
                          writes=[("YT", 12 + ct)])

                yield from proj_fm(l, ("gC", ct), TF, ev_gc)

        interleave(gen_gc(), gen_ln())
        wpw = ln512.bitcast(BF16)
        kpw = ["ln512"]
        for ct in range(4):
            for (t0, n) in chunks(TF):
                b = next_bank(2)
                for k in range(4):
                    P.add("pe", MM(PS[:, b * 512:b * 512 + n], wpw[:, k * 512 + ct * 128:k * 512 + (ct + 1) * 128],
                                   hn[:, k * TF + t0:k * TF + t0 + n], k == 0, k == 3),
                          reads=kpw + [("ar", "hn", k, t0 // 512)], writes=psq(b * 512, n))
                P.add("dve", STT(YTs(12 + ct, t0, n), PS[:, b * 512:b * 512 + n], pwb[:, ct:ct + 1], YTs(12 + ct, t0, n), ALU.add, ALU.mult),
                      reads=psq(b * 512, n) + ["pp", ("YT", 12 + ct)], writes=[("YT", 12 + ct)])

        fence()
        NX = 3
        P.add("sp", (lambda d, s: lambda e: e.dma_start(out=d, in_=s))(gpost_b, dram["gpost", l].partition_broadcast(128)),
              writes=[("ar", "gpost")], dma_key="gpost")
        P.add("dve", lambda e: e.memset(stat, 0.0), writes=["stat"] + [("stat", i) for i in range(64)], reads=["stat"])
        G = (NTF + 1) // 2
        groups = [list(range(0, G)), list(range(G, NTF))]
        ysb = hT
        P.add("dve", lambda e: e.memset(stat2[:, 62:63], 0.0), writes=ALL_HT + ["ysb_claim"])
        xslot = {}
        kept_wo = {}

        xd = [(xinD[k], ("ar", "xinD", k)) for k in range(3)]
        if (l + 1) not in layer_ids:
            xd += [(xin0[k], ("ar", "xin0", k)) for k in range(2)]
        NX = len(xd)

        def emit_xload(t):
            sl = len(xslot) % NX
            xslot[t] = sl
            P.add("sp", (lambda d, s: lambda e: e.dma_start(out=d, in_=s))(xd[sl][0], x_src[t * 128:(t + 1) * 128, :]),
                  reads=[("xsrc", l, t)], writes=[xd[sl][1]], dma_key=xd[sl][1])

        for gi, grp in enumerate(groups):
            for t in grp[:NX]:
                emit_xload(t)
            nslab = (16 * TK * 2) // 8192
            ysl = ysb[:, 0:2 * 2048 * nslab].bitcast(F32)
            sl_of = [(ti + gi * len(groups[0])) % nslab for ti in range(len(grp))]
            for cidx, cbk in enumerate((0, 1, 2, 3) if gi == 0 else (3, 2, 1, 0)):
                if gi == 1 and cbk >= 2:
                    wo, ko, io = kept_wo[cbk]
                else:
                    wo, ko, io = w_use(l, ("wo", gi, cbk))
                for ti, t in enumerate(grp):
                    b = next_bank(4)
                    for k in range(16):
                        P.add("pe", MM(PS[:, b * 512:(b + 1) * 512], YTs(k, t * 128, 128), wo[:, k * 512:(k + 1) * 512], k == 0, k == 15),
                              reads=ko + [("YT", k)], writes=psq(b * 512, 512))
                    ys = sl_of[ti]
                    ydst = ysl[:, ys * 2048 + cbk * 512:ys * 2048 + (cbk + 1) * 512]
                    P.add("act", ACT(junk0[:, 0:512], PS[:, b * 512:(b + 1) * 512], AF.Square, accum_out=stat[:, 4 * t + cbk:4 * t + cbk + 1]),
                          reads=psq(b * 512, 512) + ["stat"], writes=[("ar", "junk"), ("stat", 4 * t + cbk)])
                    P.add("act", (lambda d, s_: lambda e: e.copy(out=d, in_=s_))(ydst, PS[:, b * 512:(b + 1) * 512]),
                          reads=psq(b * 512, 512) + ["ysb_claim"], writes=[("ysb", ys, cbk)])
                    if cidx == 3:
                        sl = xslot[t]
                        st4_ = stat[:, 4 * t:4 * t + 4]
                        P.add("dve", (lambda o, s_: lambda e: e.tensor_reduce(out=o, in_=s_, axis=mybir.AxisListType.X, op=ALU.add))(stat2[:, t:t + 1], st4_),
                              reads=[("stat", 4 * t + c) for c in range(4)], writes=[("s2", t)])
                        P.add("act", ACT(stat2[:, 16 + t:17 + t], stat2[:, t:t + 1], AF.Sqrt, bias=epsc[:, 0:1], scale=1.0 / D),
                              reads=[("s2", t), "epsc"], writes=[("s2", 16 + t)])
                        P.add("dve", RCP(stat2[:, 32 + t:33 + t], stat2[:, 16 + t:17 + t]), reads=[("s2", 16 + t)],
                              writes=[("s2", 32 + t)])
                        yt = ysl[:, ys * 2048:(ys + 1) * 2048]
                        P.add("dve", STT(yt, yt, stat2[:, 32 + t:33 + t], gpost_b, ALU.mult, ALU.mult),
                              reads=[("ysb", ys, c) for c in range(4)] + [("s2", 32 + t), ("ar", "gpost")],
                              writes=[("ysb", ys, c) for c in range(4)])
                        P.add("dve", TT(yt, yt, xd[sl][0], ALU.add),
                              reads=[("ysb", ys, c) for c in range(4)] + [xd[sl][1]],
                              writes=[("ysb", ys, c) for c in range(4)])
                        if t < n_dst_tiles:
                            P.add("sp", (lambda d, s_: lambda e: e.dma_start(out=d, in_=s_))(x_dst[t * 128:(t + 1) * 128, :], yt),
                                  reads=[("ysb", ys, c) for c in range(4)], writes=[("xsrc", l + 1, t)], dma_key=("xout", ys))
                        if ti + NX < len(grp):
                            emit_xload(grp[ti + NX])
                if gi == 0 and cbk >= 2:
                    kept_wo[cbk] = (wo, ko, io)
                else:
                    w_release(io)
                if gi == 1 and cidx == 1 and (l + 1) in layer_ids:
                    nfront(0, load=False)
                    nfront(1, load=False)
            if gi == 0 and (l + 1) in layer_ids:
                npro, nfront, _, nload = make_p0(l + 1, x_dst)
                npro()
                nload(0)
                nload(1)
                p0_hoisted[l + 1] = 2
        P.add("dve", lambda e: e.memset(stat2[:, 61:62], 0.0),
              writes=ALL_HT + [("ysb", ti, c) for ti in range(8) for c in range(4)])

    with nc.allow_low_precision("bf16 matmul operands, fp32 accumulation"):
        if len(layer_ids) == 2:
            emit_layer(0, dram["x_in"], dram["x_mid"], LAYER_TILES[1][1])
            emit_layer(1, dram["x_mid"], dram["x_out"], 8)
            final_keys = [("xsrc", 2, t) for t in range(8)]
        else:
            l = layer_ids[0]
            emit_layer(l, dram["x_in"], dram["x_out"], ext_out_tiles)
            final_keys = [("xsrc", l + 1, t) for t in range(ext_out_tiles)]
        P.add("sp", None, reads=final_keys)
        P.emit()
    return nc, P


def _small(w, c0):
    blk = w[:, c0:c0 + 128]
    return blk.reshape(16, 128, 128).transpose(1, 0, 2).reshape(128, 2048)


def _big(w, c0):
    blk = w[:, c0:c0 + 512]
    b = blk.reshape(16, 128, 512).transpose(1, 0, 2).reshape(128, 8192)
    return b.reshape(128, 4, 2048).transpose(1, 0, 2)


def prep_wst(w_in, pw, w_out):
    wst = np.empty((NSLOT_DRAM, 128, 2048), np.float32)
    for h in range(NH):
        wst[4 * h + 0] = _small(w_in, 2048 + h * 128)
        wst[4 * h + 1] = _small(w_in, 1024 + h * 128)
        wst[4 * h + 2] = _small(w_in, 0 + h * 128)
        wst[4 * h + 3] = _small(w_in, 3072 + h * 128)
    wst[32:36] = _big(w_in, 4608)
    wst[36:40] = _big(w_in, 4096)
    wst[40:44] = _big(w_in, 5120)
    for ct in range(4):
        wst[44 + 2 * ct] = _small(w_in, 5632 + ct * 128)
        wst[45 + 2 * ct] = _small(w_in, 6144 + ct * 128)
        wst[52 + ct] = _small(w_in, 6656 + ct * 128)
    wst[56] = pw.reshape(4, 128, 512).transpose(1, 0, 2).reshape(128, 2048)
    for cb in range(4):
        wst[57 + 4 * cb:61 + 4 * cb] = _big(w_out, cb * 512)
    return wst


def prep_bias(rpb, half):
    out = np.full((NH, 128, 3, 5, 128), NEG, np.float32)
    loc = np.arange(128)
    for s in range(3):
        tq = s * 128 + loc
        gq = tq if half == 0 else 2047 - tq
        qr, qc = gq // 64, gq % 64
        rs = np.clip(qr - 4, 0, 24)
        cs = np.clip(qc - 8, 0, 48)
        for b in range(5):
            tk = b * 128 + loc
            gk = tk if half == 0 else 2047 - tk
            kr, kc = gk // 64, gk % 64
            KR, QR = kr[:, None], qr[None, :]
            KC, QC = kc[:, None], qc[None, :]
            ok = (KR >= rs[None, :]) & (KR < rs[None, :] + 8) & (KC >= cs[None, :]) & (KC < cs[None, :] + 16)
            dr = np.clip(KR - QR + 7, 0, 14)
            dc = np.clip(KC - QC + 15, 0, 30)
            vals = rpb[:, dr, dc]
            blk = out[:, :, s, b, :]
            blk[:, ok] = vals[:, ok]
    return out.reshape(NH, 128, 1920)


def prep_core_inputs(inp, layer_ids, half):
    m = {}
    for l in layer_ids:
        w = inp["sgu_w"][l]
        wT = w.transpose(0, 2, 1)
        bs = inp["sgu_b"][l]
        cwl = inp["conv_w"][l][:, 0, :]
        if half == 1:
            wT = wT[:, ::-1, ::-1]
            bs = bs[:, ::-1]
            cwl = cwl[::-1]
        m["wsT%d" % l] = np.ascontiguousarray(wT.transpose(1, 0, 2).reshape(128, 512))
        m["bsb%d" % l] = np.ascontiguousarray(np.repeat(bs.T[:, :, None], 128, axis=2).reshape(128, 512))
        ppa = np.empty((128, NPP), np.float32)
        ppa[:, 0:124] = cwl.T.reshape(4, 128, 31).transpose(1, 0, 2).reshape(128, 124)
        ppa[:, 124:128] = inp["conv_b"][l].reshape(4, 128).T
        ppa[:, 128:132] = inp["conv_ln_g"][l].reshape(4, 128).T
        ppa[:, 132:136] = inp["conv_ln_b"][l].reshape(4, 128).T
        ppa[:, 136:140] = inp["conv_pw_b"][l].reshape(4, 128).T
        m["pp%d" % l] = ppa
        m["ln512_%d" % l] = np.concatenate([inp["sgu_ln_g"][l], inp["sgu_ln_b"][l]])[None, :].astype(np.float32)
        m["gpre%d" % l] = np.ascontiguousarray(inp["pre_norm_g"][l][None, :])
        m["gpost%d" % l] = np.ascontiguousarray(inp["post_norm_g"][l][None, :])
        m["bias%d" % l] = prep_bias(inp["attn_rpb"][l], half)
    return m


FUSED = True
_cache = {}


def _get_prog(key, *args):
    if key not in _cache:
        _cache[key] = build_program(*args)[0]
    return _cache[key]


def kernel(**inputs):
    inp = {k: np.asarray(v, dtype=np.float32) for k, v in inputs.items()}
    x = inp["x"]
    wsts = [prep_wst(inp["w_in"][l], inp["conv_pw_w"][l], inp["w_out"][l]) for l in range(2)]
    ntk0 = LAYER_TILES[0][1]
    core_maps = []
    for c in range(8):
        b, half = c // 2, c % 2
        xs = x[b] if half == 0 else x[b, ::-1]
        m = prep_core_inputs(inp, [0, 1], half)
        m["x_in"] = np.ascontiguousarray(xs[:ntk0 * 128])
        m["wst0"], m["wst1"] = wsts[0], wsts[1]
        core_maps.append(m)
    cores = list(range(8))
    if FUSED:
        nc = _get_prog("fused", [0, 1], ntk0, 8)
        res = run_bass_kernel_spmd(nc, core_maps, core_ids=cores)
        outs = [np.asarray(r["x_out"]) for r in res.results]
    else:
        ntk1 = LAYER_TILES[1][1]
        ncA = _get_prog("L0", [0], ntk0, ntk1)
        keysA = ["x_in", "wst0", "bias0", "gpre0", "gpost0", "ln512_0", "bsb0", "wsT0", "pp0"]
        resA = run_bass_kernel_spmd(ncA, [{k: m[k] for k in keysA} for m in core_maps], core_ids=cores)
        ncB = _get_prog("L1", [1], ntk1, 8)
        keysB = ["wst1", "bias1", "gpre1", "gpost1", "ln512_1", "bsb1", "wsT1", "pp1"]
        mapsB = []
        for c in range(8):
            mm = {k: core_maps[c][k] for k in keysB}
            mm["x_in"] = np.asarray(resA.results[c]["x_out"])
            mapsB.append(mm)
        resB = run_bass_kernel_spmd(ncB, mapsB, core_ids=cores)
        outs = [np.asarray(r["x_out"]) for r in resB.results]
    out = np.empty((4, 2048, 2048), np.float32)
    for c in range(8):
        b, half = c // 2, c % 2
        if half == 0:
            out[b, 0:1024] = outs[c]
        else:
            out[b, 1024:2048] = outs[c][::-1]
    return out
```
